# Optimizing a Trainium2 kernel written in Bass

```python
import jax, jax.numpy as jnp
from jax import lax
import numpy as np

D_MODEL = 1024
BATCH = 16
SEQ = 4096
DEPTH = 1
DEC_BATCH = 8
DEC_SEQ = 16
PAST_LEN = 1024

CHUNK = 64
HEAD_DIM = 64
A_Q_HEADS = 8
A_KV_HEADS = 2
A_WINDOW = 128
A_PREV_CHUNKS = A_WINDOW // CHUNK
B_HEADS = 8
B_PREV_CHUNKS = 8
B_REACH = B_PREV_CHUNKS * CHUNK
REL_CLIP = 128
ROPE_THETA = 10000.0
D_FF = 2816
CONV_WIDTH = 3
RMS_EPS = 1e-6
NEG_INF = -1e30
A_Q_W = A_Q_HEADS * HEAD_DIM
A_KV_W = A_KV_HEADS * HEAD_DIM
B_W = B_HEADS * HEAD_DIM
IN_SPLITS = (A_Q_W, A_KV_W, A_KV_W, B_W, B_W, B_W, D_MODEL, D_MODEL)
IN_WIDTH = A_Q_W + 2 * A_KV_W + 3 * B_W + 2 * D_MODEL

kernel_name = "hybrid_stream_swa_sink_chunkrel_convglu"


def rms_norm(x, g):
    xf = x.astype(jnp.float32)
    y = xf * lax.rsqrt(jnp.mean(xf * xf, axis=-1, keepdims=True) + RMS_EPS)
    return (y * g.astype(jnp.float32)).astype(x.dtype)


def rope(x, pos):
    half = HEAD_DIM // 2
    inv = 1.0 / (ROPE_THETA ** (jnp.arange(half, dtype=jnp.float32) * (2.0 / HEAD_DIM)))
    ang = pos.astype(jnp.float32)[:, None] * inv[None, :]
    cos = jnp.cos(ang)[:, None, :]
    sin = jnp.sin(ang)[:, None, :]
    xf = x.astype(jnp.float32)
    x1, x2 = xf[..., :half], xf[..., half:]
    return jnp.concatenate([x1 * cos - x2 * sin, x2 * cos + x1 * sin], axis=-1).astype(x.dtype)


def attend(q, k, v, valid, rel, sink, rel_table):
    b, lq, hq, hd = q.shape
    hkv = k.shape[2]
    r = hq // hkv
    qg = q.reshape(b, lq, hkv, r, hd)
    s = jnp.einsum("bqgrd,bkgd->bgrqk", qg, k, preferred_element_type=jnp.float32) * (hd ** -0.5)
    if rel_table is not None:
        idx = jnp.clip(rel, -REL_CLIP, REL_CLIP) + REL_CLIP
        bias = jnp.take(rel_table.astype(jnp.float32), idx, axis=1)
        s = s + bias.reshape(hkv, r, lq, -1)[None]
    if valid is not None:
        s = jnp.where(valid, s, NEG_INF)
    if sink is not None:
        sk = jnp.broadcast_to(sink.astype(jnp.float32).reshape(1, hkv, r, 1, 1), s.shape[:-1] + (1,))
        p = jax.nn.softmax(jnp.concatenate([s, sk], axis=-1), axis=-1)[..., :-1]
    else:
        p = jax.nn.softmax(s, axis=-1)
    o = jnp.einsum("bgrqk,bkgd->bqgrd", p.astype(v.dtype), v)
    return o.reshape(b, lq, hq, hd)


def chunk_band_attention(q, k, v, n_prev, sink=None, rel_table=None):
    b, s_len, hq, hd = q.shape
    nc = s_len // CHUNK
    pad = n_prev * CHUNK
    band = pad + CHUNK
    kp = jnp.pad(k, ((0, 0), (pad, 0), (0, 0), (0, 0)))
    vp = jnp.pad(v, ((0, 0), (pad, 0), (0, 0), (0, 0)))
    kpos = jnp.arange(band) - pad
    rel = kpos[None, :] - jnp.arange(CHUNK)[:, None]
    qc = jnp.moveaxis(q.reshape(b, nc, CHUNK, hq, hd), 1, 0)

    def one_chunk(args):
        c, qb = args
        start = c * CHUNK
        kb = lax.dynamic_slice_in_dim(kp, start, band, axis=1)
        vb = lax.dynamic_slice_in_dim(vp, start, band, axis=1)
        valid = (start + kpos) >= 0
        return attend(qb, kb, vb, valid, rel, sink, rel_table)

    out = lax.map(one_chunk, (jnp.arange(nc), qc))
    return jnp.moveaxis(out, 0, 1).reshape(b, s_len, hq, hd)


def layer(x, start, cache_a_k, cache_a_v, cache_b_k, cache_b_v, conv_state,
          g_mix_pre, w_in, sinks, rel_bias, w_oa, w_ob, w_out, g_mix_post,
          g_ffn_pre, w_up, conv_w, conv_b, w_down, g_ffn_post):
    bsz, t, _ = x.shape
    h = rms_norm(x, g_mix_pre)
    offs = [int(o) for o in np.cumsum(IN_SPLITS)[:-1]]
    qa, ka, va, qb, kb, vb, ga, gb = jnp.split(h @ w_in, offs, axis=-1)
    qa = qa.reshape(bsz, t, A_Q_HEADS, HEAD_DIM)
    ka = ka.reshape(bsz, t, A_KV_HEADS, HEAD_DIM)
    va = va.reshape(bsz, t, A_KV_HEADS, HEAD_DIM)
    qb = qb.reshape(bsz, t, B_HEADS, HEAD_DIM)
    kb = kb.reshape(bsz, t, B_HEADS, HEAD_DIM)
    vb = vb.reshape(bsz, t, B_HEADS, HEAD_DIM)
    pos = start + jnp.arange(t)
    qa = rope(qa, pos)
    ka = rope(ka, pos)
    if cache_a_k is None:
        oa = chunk_band_attention(qa, ka, va, A_PREV_CHUNKS, sink=sinks)
        ob = chunk_band_attention(qb, kb, vb, B_PREV_CHUNKS, rel_table=rel_bias)
        keep_a = min(A_WINDOW, t)
        keep_b = min(B_REACH, t)
        new_rows = (ka[:, t - keep_a:], va[:, t - keep_a:], kb[:, t - keep_b:], vb[:, t - keep_b:])
        conv_prev = jnp.zeros((bsz, CONV_WIDTH - 1, D_FF), x.dtype)
    else:
        oa = attend(qa, jnp.concatenate([cache_a_k, ka], axis=1),
                    jnp.concatenate([cache_a_v, va], axis=1), None, None, sinks, None)
        lb = cache_b_k.shape[1]
        kpos = jnp.concatenate([jnp.arange(lb) - lb, jnp.arange(t)])
        rel = kpos[None, :] - jnp.arange(t)[:, None]
        ob = attend(qb, jnp.concatenate([cache_b_k, kb], axis=1),
                    jnp.concatenate([cache_b_v, vb], axis=1), None, rel, None, rel_bias)
        new_rows = (ka, va, kb, vb)
        conv_prev = conv_state.astype(x.dtype)
    mixed = (jax.nn.sigmoid(ga) * (oa.reshape(bsz, t, A_Q_W) @ w_oa)
             + jax.nn.sigmoid(gb) * (ob.reshape(bsz, t, B_W) @ w_ob))
    x = x + rms_norm(mixed @ w_out, g_mix_post)
    h = rms_norm(x, g_ffn_pre)
    u, val = jnp.split(h @ w_up, 2, axis=-1)
    up = jnp.concatenate([conv_prev, u], axis=1)
    conv = conv_b
    for j in range(CONV_WIDTH):
        conv = conv + up[:, j:j + t] * conv_w[j]
    new_conv = up[:, t:]
    ff = (jax.nn.gelu(conv, approximate=False) * val) @ w_down
    x = x + rms_norm(ff, g_ffn_post)
    return x, new_rows + (new_conv,)


def setup_inputs(seed: int = 0) -> dict:
    key = jax.random.key(seed)
    ks = jax.random.split(key, 24)

    def nrm(k, shape, scale):
        return jax.random.normal(k, shape, jnp.float32) * scale

    a_rows = min(A_WINDOW, PAST_LEN)
    b_rows = min(B_REACH, PAST_LEN)
    return {
        "x_prompt": nrm(ks[0], (BATCH, SEQ, D_MODEL), 1.0),
        "x_sample": nrm(ks[1], (DEC_BATCH, DEC_SEQ, D_MODEL), 1.0),
        "cache_a_k": nrm(ks[2], (DEPTH, DEC_BATCH, a_rows, A_KV_HEADS, HEAD_DIM), 1.0),
        "cache_a_v": nrm(ks[3], (DEPTH, DEC_BATCH, a_rows, A_KV_HEADS, HEAD_DIM), 1.0),
        "cache_b_k": nrm(ks[4], (DEPTH, DEC_BATCH, b_rows, B_HEADS, HEAD_DIM), 1.0),
        "cache_b_v": nrm(ks[5], (DEPTH, DEC_BATCH, b_rows, B_HEADS, HEAD_DIM), 1.0),
        "state_conv": nrm(ks[6], (DEPTH, DEC_BATCH, CONV_WIDTH - 1, D_FF), 1.0),
        "g_mix_pre": 1.0 + nrm(ks[7], (DEPTH, D_MODEL), 0.1),
        "w_in": nrm(ks[8], (DEPTH, D_MODEL, IN_WIDTH), D_MODEL ** -0.5),
        "sinks": nrm(ks[9], (DEPTH, A_Q_HEADS), 1.0),
        "rel_bias": nrm(ks[10], (DEPTH, B_HEADS, 2 * REL_CLIP + 1), 0.5),
        "w_oa": nrm(ks[11], (DEPTH, A_Q_W, D_MODEL), A_Q_W ** -0.5),
        "w_ob": nrm(ks[12], (DEPTH, B_W, D_MODEL), B_W ** -0.5),
        "w_out": nrm(ks[13], (DEPTH, D_MODEL, D_MODEL), D_MODEL ** -0.5),
        "g_mix_post": 1.0 + nrm(ks[14], (DEPTH, D_MODEL), 0.1),
        "g_ffn_pre": 1.0 + nrm(ks[15], (DEPTH, D_MODEL), 0.1),
        "w_up": nrm(ks[16], (DEPTH, D_MODEL, 2 * D_FF), D_MODEL ** -0.5),
        "conv_w": nrm(ks[17], (DEPTH, CONV_WIDTH, D_FF), CONV_WIDTH ** -0.5),
        "conv_b": nrm(ks[18], (DEPTH, D_FF), 0.02),
        "w_down": nrm(ks[19], (DEPTH, D_FF, D_MODEL), D_FF ** -0.5),
        "g_ffn_post": 1.0 + nrm(ks[20], (DEPTH, D_MODEL), 0.1),
    }


def reference(x_prompt, x_sample, cache_a_k, cache_a_v, cache_b_k, cache_b_v, state_conv,
              g_mix_pre, w_in, sinks, rel_bias, w_oa, w_ob, w_out, g_mix_post,
              g_ffn_pre, w_up, conv_w, conv_b, w_down, g_ffn_post):
    yp = x_prompt
    ys = x_sample
    new_p = []
    new_s = []
    for l in range(DEPTH):
        w = (g_mix_pre[l], w_in[l], sinks[l], rel_bias[l], w_oa[l], w_ob[l], w_out[l],
             g_mix_post[l], g_ffn_pre[l], w_up[l], conv_w[l], conv_b[l], w_down[l], g_ffn_post[l])
        yp, sp = layer(yp, 0, None, None, None, None, None, *w)
        ys, ss = layer(ys, PAST_LEN, cache_a_k[l], cache_a_v[l], cache_b_k[l], cache_b_v[l],
                       state_conv[l], *w)
        new_p.append(sp)
        new_s.append(ss)

    def stack(lst, i):
        return jnp.stack([s[i] for s in lst], axis=0)

    return (yp, ys,
            stack(new_p, 0), stack(new_p, 1), stack(new_p, 2), stack(new_p, 3), stack(new_p, 4),
            stack(new_s, 0), stack(new_s, 1), stack(new_s, 2), stack(new_s, 3), stack(new_s, 4))
```

```python
from contextlib import ExitStack

import numpy as np
import ml_dtypes

import concourse.bass as bass
import concourse.mybir as mybir
from concourse.bass_utils import run_bass_kernel_spmd

F32 = mybir.dt.float32
BF16 = mybir.dt.bfloat16
AF = mybir.ActivationFunctionType
ALU = mybir.AluOpType

D = 1024
DFF = 2816
NFT = 22
EPS = 1e-6
T1 = 512
T2 = 512
NC1 = 2304
QA, KA, QB, KB, VB, VA = 0, 512, 640, 1152, 1664, 2176
ROT = 8000
ENG = ("pe", "act", "dve", "pool", "sp")


class Buf:
    __slots__ = ("w", "r", "excl")

    def __init__(self, excl=False):
        self.w = None
        self.r = {}
        self.excl = excl


class T:
    def __init__(self, ap, bufs=None):
        self.ap = ap
        self.bufs = bufs if bufs is not None else [Buf()]


def _bufs(items):
    out = []
    for it in items:
        if isinstance(it, T):
            out.extend(it.bufs)
        elif isinstance(it, Buf):
            out.append(it)
        else:
            out.extend(_bufs(it))
    return out


class Sched:
    def __init__(self):
        self.streams = {e: [] for e in ENG}
        self.cnt = {e: 0 for e in ENG}
        self.seen = {e: {} for e in ENG}
        self.chan = {}
        self.dead = False
        self.ckpt = 0
        import os as _os
        self.limit = int(_os.environ.get("KLIMIT", "0"))

    def sub(self, name):
        import os as _os
        if _os.environ.get("KSUB", "") == name:
            self.dead = True

    def checkpoint(self):
        self.ckpt += 1
        if self.limit and self.ckpt >= self.limit:
            self.dead = True

    def _waits(self, eng, reads, writes):
        d = {}
        for b in reads:
            if b.w is not None and b.w[1] > d.get(b.w[0], 0):
                d[b.w[0]] = b.w[1]
            if b.excl:
                for k, v in b.r.items():
                    if k != ("e", eng) and v > d.get(k, 0):
                        d[k] = v
        for b in writes:
            if b.w is not None and b.w[1] > d.get(b.w[0], 0):
                d[b.w[0]] = b.w[1]
            for k, v in b.r.items():
                if v > d.get(k, 0):
                    d[k] = v
        seen = self.seen[eng]
        waits = []
        for k, v in d.items():
            if eng == "pe" and k == ("e", "pe"):
                continue
            if v > seen.get(k, 0):
                waits.append((k, v))
                seen[k] = v
        return waits

    def _mark(self, key, val, reads, writes):
        for b in writes:
            b.w = (key, val)
            b.r = {}
        for b in reads:
            if b in writes:
                continue
            if val > b.r.get(key, 0):
                b.r[key] = val

    def op(self, eng, fn, reads=(), writes=()):
        if self.dead:
            return 0
        reads = _bufs(reads)
        writes = _bufs(writes)
        waits = self._waits(eng, reads, writes)
        self.cnt[eng] += 1
        idx = self.cnt[eng]
        self.streams[eng].append(("op", waits, fn, None))
        self._mark(("e", eng), idx, reads, writes)
        return idx

    def dma(self, chan, fn, reads=(), writes=(), queue="sp"):
        if self.dead:
            return
        reads = _bufs(reads)
        writes = _bufs(writes)
        waits = self._waits(queue, reads, writes)
        prev = self.chan.get(chan, 0)
        key = ("d", chan)
        if prev > self.seen[queue].get(key, 0):
            waits.append((key, prev))
            self.seen[queue][key] = prev
        self.chan[chan] = prev + 1
        self.streams[queue].append(("dma", waits, fn, chan))
        self._mark(key, prev + 1, reads, writes)

    def barrier(self):
        if self.dead:
            return
        for e in ENG:
            waits = []
            for e2 in ENG:
                if e2 == "sp":
                    continue
                v = self.cnt[e2]
                if v > self.seen[e].get(("e", e2), 0) and not (e == "pe" and e2 == "pe"):
                    waits.append((("e", e2), v))
                    self.seen[e][("e", e2)] = v
            for c, v in self.chan.items():
                if v > self.seen[e].get(("d", c), 0):
                    waits.append((("d", c), v))
                    self.seen[e][("d", c)] = v
            if waits:
                self.streams[e].append(("wait", waits, None, None))

    def final_wait(self):
        waits = []
        for c, v in self.chan.items():
            waits.append((("d", c), v))
        for e2 in ENG:
            if e2 != "sp" and self.cnt[e2] > 0:
                waits.append((("e", e2), self.cnt[e2]))
        self.streams["sp"].append(("wait", waits, None, None))

    def emit(self, nc, stack):
        sems = {}
        for e in ENG:
            if e == "sp":
                continue
            n = (self.cnt[e] + ROT - 1) // ROT
            for k in range(max(n, 1)):
                sems[("e", e, k)] = stack.enter_context(nc.semaphore(f"s_{e}_{k}"))
        for c in self.chan:
            sems[("d", c)] = stack.enter_context(nc.semaphore(f"d_{c}"))

        def do_wait(eobj, k, v):
            if k[0] == "e":
                kk = (v - 1) // ROT
                eobj.wait_ge(sems[("e", k[1], kk)], (v - 1) % ROT + 1)
            else:
                eobj.wait_ge(sems[("d", k[1])], 16 * v)

        def replay(ename, eobj):
            idx = 0
            for kind, waits, fn, chan in self.streams[ename]:
                for k, v in waits:
                    do_wait(eobj, k, v)
                if kind == "op":
                    idx += 1
                    ins = fn(eobj)
                    ins.then_inc(sems[("e", ename, (idx - 1) // ROT)], 1)
                elif kind == "dma":
                    ins = fn(eobj)
                    ins.then_inc(sems[("d", chan)], 16)

        block = stack.enter_context(nc.Block())

        @block.tensor
        def _(e):
            replay("pe", e)

        @block.scalar
        def _(e):
            replay("act", e)

        @block.vector
        def _(e):
            replay("dve", e)

        @block.gpsimd
        def _(e):
            replay("pool", e)

        @block.sync
        def _(e):
            replay("sp", e)


class Arena:
    def __init__(self, ap, nwords):
        self.ap = ap
        self.n = nwords
        self.off = 0

    def f32(self, n, parts=128):
        self.off = (self.off + 15) // 16 * 16
        assert self.off + n <= self.n, ("arena overflow", self.off + n, self.n)
        r = self.ap[0:parts, self.off:self.off + n]
        self.off += n
        return r

    def bf16(self, n, parts=128):
        assert n % 2 == 0
        return self.f32(n // 2, parts).bitcast(BF16)


def v3(ap, a):
    return ap.rearrange("p (a b) -> p a b", a=a)


def v4(ap, a, b):
    return ap.rearrange("p (a b c) -> p a b c", a=a, b=b)


def build_nc(S, NSEQ, sample=True):
    assert S % T1 == 0
    nc = bass.Bass("TRN2", target_bir_lowering=False)
    s = Sched()

    def din(name, shape, dt=F32):
        return nc.dram_tensor(name, list(shape), dt, kind="ExternalInput").ap()

    def dout(name, shape):
        return nc.dram_tensor(name, list(shape), F32, kind="ExternalOutput").ap()

    def dint(name, shape, dt=F32):
        return nc.dram_tensor(name, list(shape), dt).ap()

    KA_ROWS = min(128, S)
    KB_ROWS = min(512, S)
    d_xp = din("xp", [NSEQ, S, D])
    d_xs = din("xs", [16, D])
    d_cak = din("cak", [128, 128])
    d_cav = din("cav", [128, 128])
    d_cbk = din("cbk", [512, 512])
    d_cbv = din("cbv", [512, 512])
    d_sconv = din("sconv", [128, NFT * 2])
    d_win = din("w_in", [D, 4352])
    d_woa = din("w_oa", [512, D])
    d_wob = din("w_ob", [512, D])
    d_wout = din("w_out", [D, D])
    d_wup = din("w_up", [D, 2 * DFF])
    d_wdn = din("w_down", [DFF, D])
    d_gpre = din("gpre", [128, 8])
    d_gpost = din("gpost", [1, D])
    d_g2pre = din("g2pre", [128, 8])
    d_g2post = din("g2post", [1, D])
    d_sinks = din("sinks", [1, 8])
    d_trev = din("trev", [8, 257])
    d_cw = din("cw", [128, NFT * 3])
    d_cb = din("cb", [128, NFT])
    d_ident = din("ident", [128, 128], BF16)
    d_jmat = din("jmat", [128, 128], BF16)
    d_perms = din("perms", [128, 5 * 128], BF16)
    d_ea = din("ea", [128, 512], BF16)
    d_ropefm = din("ropefm", [128, 2, S])
    d_ropefm_s = din("ropefm_s", [128, 2, 16])
    d_ropetm = din("ropetm", [128, 64])
    d_ropetm_s = din("ropetm_s", [16, 64])
    o_yp = dout("yp", [NSEQ, S, D])
    o_ys = dout("ys", [16, D])
    o_akp = dout("akp", [NSEQ, KA_ROWS, 128])
    o_avp = dout("avp", [NSEQ, KA_ROWS, 128])
    o_bkp = dout("bkp", [NSEQ, KB_ROWS, 512])
    o_bvp = dout("bvp", [NSEQ, KB_ROWS, 512])
    o_convp = dout("convp", [NSEQ, 2, DFF])
    o_aks = dout("aks", [16, 128])
    o_avs = dout("avs", [16, 128])
    o_bks = dout("bks", [16, 512])
    o_bvs = dout("bvs", [16, 512])
    o_convs = dout("convs", [2, DFF])
    x_x1p = dint("x1p", [NSEQ, S, D])
    x_x1s = dint("x1s", [16, D])
    x_wf = dint("wf", [8, 128, 3072], BF16)
    x_ptab = dint("ptab", [8, 768])
    b_x1 = Buf()
    b_wf = Buf()
    b_ptab = Buf()

    stack = ExitStack()
    with stack:
        NW = 53200
        arena_t = stack.enter_context(nc.sbuf_tensor("arena", [128, NW], F32))
        psum_t = stack.enter_context(nc.psum_tensor("ps", [128, 4096], F32))
        A = Arena(arena_t, NW)
        pbank = [T(psum_t[:, 512 * i:512 * (i + 1)], [Buf(excl=True)]) for i in range(8)]

        def ppair(i):
            return T(psum_t[:, 1024 * i:1024 * (i + 1)], pbank[2 * i].bufs + pbank[2 * i + 1].bufs)

        ppairs = [ppair(i) for i in range(4)]
        pT = [T(psum_t[:, 512 * i:512 * (i + 1)].bitcast(BF16), pbank[i].bufs) for i in (6, 7)]

        ident = T(A.bf16(128))
        jmat = T(A.bf16(128))
        perms = T(v3(A.bf16(5 * 128), 5))
        NSS = 544
        ss_ap = A.f32(NSS)
        rs_ap = A.f32(NSS)
        mhalf = T(A.f32(1))
        ss_i = [0]

        def new_ss():
            i = ss_i[0]
            ss_i[0] += 2
            assert i + 1 < NSS
            return T(ss_ap[:, i:i + 2]), T(rs_ap[:, i:i + 1])

        s.dma("c0", lambda e: e.dma_start(out=ident.ap, in_=d_ident[:, :]), writes=[ident])
        s.dma("c1", lambda e: e.dma_start(out=jmat.ap, in_=d_jmat[:, :]), writes=[jmat])
        s.dma("c2", lambda e: e.dma_start(out=perms.ap, in_=v3(d_perms, 5)), writes=[perms])
        ssall = T(ss_ap)
        s.op("pool", lambda e: e.memset(ss_ap, 0.0), writes=[ssall])
        s.op("pool", lambda e: e.memset(mhalf.ap, -0.5), writes=[mhalf])
        base_off = A.off

        misc_i = [0]

        def mchan():
            misc_i[0] += 1
            return f"m{misc_i[0] % 6}"

        def emit_rstd(ssT, rsT, n, ncols=2):
            if ncols == 2:
                s.op("dve", lambda e: e.tensor_tensor(out=rsT.ap[0:n], in0=ssT.ap[0:n, 0:1], in1=ssT.ap[0:n, 1:2], op=ALU.add),
                     reads=[ssT], writes=[rsT])
                src = rsT
            else:
                src = ssT
            s.op("dve", lambda e: e.tensor_scalar(out=rsT.ap[0:n], in0=src.ap[0:n, 0:1], scalar1=1.0 / D, scalar2=EPS,
                                                  op0=ALU.mult, op1=ALU.add), reads=[src], writes=[rsT])
            s.op("pool", lambda e: e.tensor_tensor(out=rsT.ap[0:n], in0=rsT.ap[0:n], in1=mhalf.ap[0:n], op=ALU.pow),
                 reads=[rsT, mhalf], writes=[rsT])

        w1 = T(v3(A.bf16(8 * NC1), 8))
        wout = T(v3(A.bf16(8 * D), 8))
        gpost = T(A.f32(D))
        e_b = T(v4(A.bf16(8 * 5 * 128), 8, 5))
        e_a = T(v4(A.bf16(512), 2, 2))
        esink = T(A.f32(8))
        gpre = T(A.f32(8))
        ngpre = T(A.f32(8))
        act_off = A.off

        st = [T(A.f32(4352)) for _ in range(4)]
        tb16 = [T(A.bf16(2048)) for _ in range(2)]
        trev_sb = T(A.f32(257, parts=8))
        ptab_sb = T(A.f32(768, parts=8))
        epre = T(A.f32(640))
        eexp = T(A.bf16(640))
        assert A.off <= NW

        s.dma(mchan(), lambda e: e.dma_start(out=gpre.ap, in_=d_gpre[:, :]), writes=[gpre])
        s.dma(mchan(), lambda e: e.dma_start(out=gpost.ap, in_=d_gpost[0:1, :].partition_broadcast(128)), writes=[gpost])
        s.dma(mchan(), lambda e: e.dma_start(out=esink.ap, in_=d_sinks[0:1, :].partition_broadcast(128)), writes=[esink])
        s.dma(mchan(), lambda e: e.dma_start(out=e_a.ap, in_=v4(d_ea, 2, 2)), writes=[e_a])
        s.dma(mchan(), lambda e: e.dma_start(out=trev_sb.ap, in_=d_trev[:, :]), writes=[trev_sb])
        s.op("dve", lambda e: e.tensor_scalar(out=ngpre.ap, in0=gpre.ap, scalar1=-1.0, scalar2=None, op0=ALU.mult),
             reads=[gpre], writes=[ngpre])
        s.op("act", lambda e: e.activation(out=esink.ap, in_=esink.ap, func=AF.Exp), reads=[esink], writes=[esink])
        s.op("dve", lambda e: e.tensor_copy(out=ptab_sb.ap[:, 0:256], in_=trev_sb.ap[:, 1:257]), reads=[trev_sb], writes=[ptab_sb])
        s.op("dve", lambda e: e.tensor_copy(out=ptab_sb.ap[:, 256:768], in_=trev_sb.ap[:, 256:257].to_broadcast([8, 512])),
             reads=[trev_sb], writes=[ptab_sb])
        s.dma(mchan(), lambda e: e.dma_start(out=x_ptab[:, :], in_=ptab_sb.ap), reads=[ptab_sb], writes=[b_ptab])

        s.checkpoint()
        cv_i = [0]
        nodep = [w1, wout]

        def cvt(out_ap, in_ap, scal, reads, writes):
            writes = [w for w in writes if w not in nodep]
            cv_i[0] += 1
            if cv_i[0] % 2:
                s.op("dve", lambda e: e.tensor_scalar(out=out_ap, in0=in_ap, scalar1=scal, scalar2=None, op0=ALU.mult),
                     reads=reads, writes=writes)
            else:
                s.op("act", lambda e: e.activation(out=out_ap, in_=in_ap, func=AF.Copy, scale=scal), reads=reads, writes=writes)

        for kc in range(8):
            stg = st[kc % 4]
            tbb = tb16[kc % 2]
            s.dma(f"st{kc % 4}", lambda e, stg=stg, kc=kc: e.dma_start(out=stg.ap, in_=d_win[kc * 128:(kc + 1) * 128, :]),
                  writes=[stg])
            g = gpre.ap[:, kc:kc + 1]
            ng = ngpre.ap[:, kc:kc + 1]
            sa = stg.ap
            wk = w1.ap[:, kc, :]
            rd = [stg, gpre, ngpre]
            cvt(wk[:, QA:QA + 512], sa[:, 0:512], g, rd, [w1])
            cvt(wk[:, KA:KA + 128], sa[:, 512:640], g, rd, [w1])
            cvt(wk[:, QB:QB + 512], sa[:, 768:1280], g, rd, [w1])
            cvt(wk[:, KB:KB + 512], sa[:, 1280:1792], g, rd, [w1])
            cvt(wk[:, VB:VB + 512], sa[:, 1792:2304], g, rd, [w1])
            cvt(wk[:, VA:VA + 128], sa[:, 640:768], g, rd, [w1])
            cvt(tbb.ap, sa[:, 2304:4352], g, rd, [tbb])
            for gi in range(2):
                dst = bass.AP(x_wf.tensor, (8 + 8 * gi + kc) * 128, [[3072, 128], [128 * 3072, 8], [1, 128]])
                src = v3(tbb.ap[:, gi * 1024:(gi + 1) * 1024], 8)
                s.dma(mchan(), lambda e, dst=dst, src=src: e.dma_start(out=dst, in_=src), reads=[tbb], writes=[b_wf])
        s.checkpoint()
        for wi, dw in enumerate((d_woa, d_wob)):
            for kc in range(4):
                i = wi * 4 + kc
                stg = st[i % 4]
                tbb = tb16[i % 2]
                s.dma(f"st{i % 4}", lambda e, stg=stg, dw=dw, kc=kc: e.dma_start(out=stg.ap[:, 0:D], in_=dw[kc * 128:(kc + 1) * 128, :]),
                      writes=[stg])
                cvt(tbb.ap[:, 0:D], stg.ap[:, 0:D], 1.0, [stg], [tbb])
                dst = bass.AP(x_wf.tensor, (wi * 4 + kc) * 128, [[3072, 128], [128 * 3072, 8], [1, 128]])
                src = v3(tbb.ap[:, 0:D], 8)
                s.dma(mchan(), lambda e, dst=dst, src=src: e.dma_start(out=dst, in_=src), reads=[tbb], writes=[b_wf])
        for kc in range(8):
            stg = st[kc % 4]
            s.dma(f"st{kc % 4}", lambda e, stg=stg, kc=kc: e.dma_start(out=stg.ap[:, 0:D], in_=d_wout[kc * 128:(kc + 1) * 128, :]),
                  writes=[stg])
            cvt(wout.ap[:, kc, :], stg.ap[:, 0:D], 0.5, [stg], [wout])
        s.checkpoint()
        for h in range(8):
            src = bass.AP(x_ptab.tensor, h * 768, [[1, 128], [128, 5], [1, 128]])
            s.dma(mchan(), lambda e, src=src: e.dma_start(out=v3(epre.ap, 5), in_=src), reads=[b_ptab], writes=[epre])
            s.op("act", lambda e: e.activation(out=eexp.ap, in_=epre.ap, func=AF.Exp), reads=[epre], writes=[eexp])
            pp = ppairs[h % 2]
            s.op("pe", lambda e, pp=pp: e.matmul(pp.ap[:, 0:512], lhsT=jmat.ap, rhs=eexp.ap[:, 0:512], start=True, stop=True),
                 reads=[jmat, eexp], writes=[pp])
            s.op("pe", lambda e, pp=pp: e.matmul(pp.ap[:, 512:640], lhsT=jmat.ap, rhs=eexp.ap[:, 512:640], start=True, stop=True),
                 reads=[jmat, eexp], writes=[pp])
            for eb in range(5):
                s.op("dve", lambda e, pp=pp, h=h, eb=eb: e.tensor_copy(out=e_b.ap[:, h, eb, :], in_=pp.ap[:, (4 - eb) * 128:(5 - eb) * 128]),
                     reads=[pp], writes=[e_b])
        s.op("pool", lambda e: e.memset(e_b.ap[0:64, :, 0, 64:128], 0.0), writes=[e_b])
        s.op("pool", lambda e: e.memset(e_b.ap[64:128, :, 4, 0:64], 0.0), writes=[e_b])
        s.checkpoint()
        s.barrier()

        A.off = act_off
        xin = [T(A.f32(D)) for _ in range(2)]
        xres_off = A.off
        xres = [T(A.f32(D)) for _ in range(2)]
        cstg = T(arena_t[:, xres_off:xres_off + 2048], xres[0].bufs + xres[1].bufs)
        cstb = T(xin[0].ap.bitcast(BF16), xin[0].bufs)
        junkT = T(A.bf16(D))
        junk = junkT.ap
        xsb = [T(A.bf16(D)) for _ in range(2)]
        hT = T(v3(A.bf16(8 * T1), 8))
        ropeT = T(v3(A.f32(2 * T1), 2))
        qaT = T(v3(A.bf16(4 * T1), 4))
        qbT = T(v3(A.bf16(4 * T1), 4))
        kaT_ap = v3(A.bf16(2 * 1024), 2)
        kbT_ap = v3(A.bf16(4 * 1024), 4)
        va_ap = v4(A.bf16(8 * 2 * 66), 8, 2)
        vb_ap = v4(A.bf16(8 * 8 * 66), 8, 8)
        ring = [dict(ka=Buf(), kb=Buf(), va=Buf(), vb=Buf()) for _ in range(8)]
        qraw = [T(A.bf16(T1)) for _ in range(2)]
        rtmp_off = A.off
        rtmp = [T(A.f32(T1)) for _ in range(2)]
        tmpz = T(arena_t[:, rtmp_off:rtmp_off + D], rtmp[0].bufs + rtmp[1].bufs)
        pB = [T(v3(A.bf16(640), 5)) for _ in range(3)]
        pA = [T(v4(A.bf16(512), 2, 2)) for _ in range(2)]
        otok = [T(A.bf16(D)) for _ in range(1)]
        rden = [T(A.f32(8)) for _ in range(4)]
        oT = T(v3(A.bf16(8 * T1), 8))
        tab = [T(A.f32(T1)) for _ in range(2)]
        t12 = [T(A.f32(T1)) for _ in range(2)]
        mT = T(v3(A.bf16(8 * T1), 8))
        wfs = [T(v3(A.bf16(3072), 24)) for _ in range(2)]
        ost = [T(A.f32(640)) for _ in range(2)]
        assert A.off <= NW, A.off
        p1_end = A.off

        s.op("pool", lambda e: e.memset(va_ap[:, :, :, 64:65], 1.0), writes=[r["va"] for r in ring])
        s.op("pool", lambda e: e.memset(vb_ap[:, :, :, 64:65], 1.0), writes=[r["vb"] for r in ring])

        cnt = dict(xin=0, xres=0, pb=0, pa=0, otok=0, rden=0, wf=0, ost=0, pt=0, bank=0, xsb=0, qraw=0, rbank=0)

        def rr(name, n):
            v = cnt[name] % n
            cnt[name] += 1
            return v

        def transpose_to(dst_ap, src_T, src_cols, nq, evac_eng, dstT):
            pt = pT[rr("pt", 2)]
            ptv = v3(pt.ap, 8)
            for k in range(8):
                s.op("pe", lambda e, k=k, ptv=ptv: e.transpose(out=ptv[:, k, 0:nq], in_=src_cols(k), identity=ident.ap[0:nq, 0:nq]),
                     reads=[src_T, ident], writes=[pt])
            if evac_eng == "act":
                s.op("act", lambda e, ptv=ptv: e.activation(out=dst_ap, in_=ptv[:, :, 0:nq], func=AF.Copy), reads=[pt], writes=[dstT])
            else:
                s.op("dve", lambda e, ptv=ptv: e.tensor_copy(out=dst_ap, in_=ptv[:, :, 0:nq]), reads=[pt], writes=[dstT])

        def p1_load_x(p0, j):
            xi = xin[rr("xin", 2)]
            p0["xi"][j] = xi
            s.dma(f"xin{cnt['xin'] % 2}", lambda e: e.dma_start(out=xi.ap[0:p0["nq"], :], in_=p0["xsrc"](j)), writes=[xi])

        def p1_stage0a(p0, j):
            nq = p0["nq"]
            xi = p0["xi"][j]
            xb = xsb[rr("xsb", 2)]
            p0["xb"][j] = xb
            ssT, rsT = new_ss()
            s.op("act", lambda e: e.activation(out=xb.ap[0:nq, :], in_=xi.ap[0:nq, :], func=AF.Square, accum_out=ssT.ap[0:nq, 0:1]),
                 reads=[xi], writes=[ssT, xb])
            emit_rstd(ssT, rsT, nq, ncols=1)
            s.op("dve", lambda e: e.tensor_scalar(out=xb.ap[0:nq, :], in0=xi.ap[0:nq, :], scalar1=rsT.ap[0:nq], scalar2=None, op0=ALU.mult),
                 reads=[xi, rsT], writes=[xb])
            if j + 2 < p0["nsub"]:
                p1_load_x(p0, j + 2)

        def p1_stage0b(p0, j):
            nq = p0["nq"]
            xb = p0["xb"][j]
            transpose_to(hT.ap[:, :, j * nq:(j + 1) * nq], xb, lambda k: xb.ap[0:nq, k * 128:(k + 1) * 128], nq, "act", hT)

        def p1_tile(xsrc, x1dst, nsub, nq, rope_src, blocksA, blocksB, newA, newB, outs, pre_out=None, per_out=None, mid_out=None):
            Tt = nsub * nq
            s.dma("rope", lambda e: e.dma_start(out=ropeT.ap[:, :, 0:Tt], in_=rope_src), writes=[ropeT])
            s.checkpoint()
            def proj_fm(col0, bank):
                for kc in range(8):
                    s.op("pe", lambda e, kc=kc: e.matmul(bank.ap[:, 0:Tt], lhsT=w1.ap[:, kc, col0:col0 + 128], rhs=hT.ap[:, kc, 0:Tt],
                                                         start=(kc == 0), stop=(kc == 7)), reads=[w1, hT], writes=[bank])

            def rope_finish(b1, b2, dst_list):
                s.op("dve", lambda e: e.tensor_tensor(out=rtmp[0].ap[:, 0:Tt], in0=b1.ap[:, 0:Tt], in1=ropeT.ap[:, 0, 0:Tt], op=ALU.mult),
                     reads=[b1, ropeT], writes=[rtmp[0]])
                s.op("dve", lambda e: e.tensor_tensor(out=rtmp[1].ap[:, 0:Tt], in0=b2.ap[:, 0:Tt], in1=ropeT.ap[:, 1, 0:Tt], op=ALU.mult),
                     reads=[b2, ropeT], writes=[rtmp[1]])
                for dst_ap, c0, n, bufs in dst_list:
                    s.op("dve", lambda e, dst_ap=dst_ap, c0=c0, n=n: e.tensor_tensor(out=dst_ap, in0=rtmp[0].ap[:, c0:c0 + n], in1=rtmp[1].ap[:, c0:c0 + n],
                                                                                  op=ALU.add), reads=[rtmp[0], rtmp[1]], writes=bufs)

            def raw_tile(col0):
                b1 = pbank[rr("rbank", 8)]
                proj_fm(col0, b1)
                qr = qraw[rr("qraw", 2)]
                s.op("act", lambda e: e.activation(out=qr.ap[:, 0:Tt], in_=b1.ap[:, 0:Tt], func=AF.Copy), reads=[b1], writes=[qr])
                return b1, qr

            def perm_mm(pi, qr):
                b2 = pbank[rr("rbank", 8)]
                s.op("pe", lambda e: e.matmul(b2.ap[:, 0:Tt], lhsT=perms.ap[:, pi, :], rhs=qr.ap[:, 0:Tt], start=True, stop=True),
                     reads=[perms, qr], writes=[b2])
                return b2

            prev = None
            for i in range(5):
                cur = raw_tile(QA + i * 128) if i < 4 else None
                if prev is not None:
                    pi_, (pb1, pqr) = prev
                    b2 = perm_mm(0, pqr)
                    rope_finish(pb1, b2, [(qaT.ap[:, pi_, 0:Tt], 0, Tt, [qaT])])
                prev = (i, cur) if cur is not None else None
            s.sub('qa')
            kb1, kqr = raw_tile(KA)
            for g in range(2):
                bd = perm_mm(1 + g, kqr)
                br = perm_mm(3 + g, kqr)
                rope_finish(bd, br, [(kaT_ap[:, g, newA(j) * 128:newA(j) * 128 + nq], j * nq, nq, [ring[newA(j)]["ka"]]) for j in range(nsub)])
            s.sub('ka')
            for i in range(4):
                b1 = pbank[rr("bank", 6)]
                proj_fm(QB + i * 128, b1)
                s.op("act", lambda e, i=i, b1=b1: e.activation(out=qbT.ap[:, i, 0:Tt], in_=b1.ap[:, 0:Tt], func=AF.Copy),
                     reads=[b1], writes=[qbT])
            for i in range(4):
                b1 = pbank[rr("bank", 6)]
                proj_fm(KB + i * 128, b1)
                for j in range(nsub):
                    rb = newB(j)
                    s.op("act", lambda e, i=i, j=j, rb=rb, b1=b1: e.activation(out=kbT_ap[:, i, rb * 128:rb * 128 + nq],
                                                                               in_=b1.ap[:, j * nq:(j + 1) * nq], func=AF.Copy),
                         reads=[b1], writes=[ring[rb]["kb"]])
            s.sub('qkb')
            for j in range(nsub):
                pp = ppairs[rr("bank", 6) % 3]
                for (c0, n, o0) in ((VB, 512, 0), (VA, 128, 512)):
                    for kc in range(8):
                        s.op("pe", lambda e, kc=kc, c0=c0, n=n, o0=o0, pp=pp, j=j: e.matmul(
                            pp.ap[0:nq, o0:o0 + n], lhsT=hT.ap[:, kc, j * nq:(j + 1) * nq], rhs=w1.ap[:, kc, c0:c0 + n],
                            start=(kc == 0), stop=(kc == 7)), reads=[w1, hT], writes=[pp])
                rbA, rbB = newA(j), newB(j)
                s.op("dve", lambda e, pp=pp, rbB=rbB: e.tensor_copy(out=vb_ap[0:nq, rbB, :, 0:64], in_=v3(pp.ap[0:nq, 0:512], 8)),
                     reads=[pp], writes=[ring[rbB]["vb"]])
                s.op("dve", lambda e, pp=pp, rbA=rbA: e.tensor_copy(out=va_ap[0:nq, rbA, :, 0:64], in_=v3(pp.ap[0:nq, 512:640], 2)),
                     reads=[pp], writes=[ring[rbA]["va"]])
                s.sub('v')
                if outs is not None:
                    so = ost[rr("ost", 2)]
                    import os as _os
                    if _os.environ.get("KVAR", "") == "dve":
                        s.op("dve", lambda e, pp=pp, so=so: e.tensor_copy(out=so.ap[0:nq, 0:512], in_=pp.ap[0:nq, 0:512]), reads=[pp], writes=[so])
                        s.op("dve", lambda e, pp=pp, so=so: e.tensor_copy(out=so.ap[0:nq, 512:640], in_=pp.ap[0:nq, 512:640]), reads=[pp], writes=[so])
                    elif _os.environ.get("KVAR", "") == "low":
                        s.op("act", lambda e, pp=pp, so=so: e.activation(out=junkT.ap.bitcast(F32)[0:nq, 0:512], in_=pp.ap[0:nq, 0:512], func=AF.Copy),
                             reads=[pp], writes=[junkT])
                    else:
                        kv = _os.environ.get("KVAR", "")
                        fn_ = AF.Identity if kv == "ident" else AF.Copy
                        if kv != "second":
                            s.op("act", lambda e, pp=pp, so=so: e.activation(out=so.ap[0:nq, 0:512], in_=pp.ap[0:nq, 0:512], func=fn_),
                                 reads=[pp], writes=[so])
                        if kv != "first":
                            s.op("act", lambda e, pp=pp, so=so: e.activation(out=so.ap[0:nq, 512:640], in_=pp.ap[0:nq, 512:640], func=fn_),
                                 reads=[pp], writes=[so])
                    s.sub('vcopy')
                    dbv = outs["bv"](j)
                    if dbv is not None:
                        s.dma(mchan(), lambda e, so=so, dbv=dbv: e.dma_start(out=dbv, in_=so.ap[0:nq, 0:512]), reads=[so])
                    s.sub('vdma1')
                    dav = outs["av"](j)
                    if dav is not None:
                        s.dma(mchan(), lambda e, so=so, dav=dav: e.dma_start(out=dav, in_=so.ap[0:nq, 512:640]), reads=[so])
                    s.sub('vout')
                    dbk = outs["bk"](j)
                    if dbk is not None:
                        b1 = pbank[rr("bank", 6)]
                        for kc in range(8):
                            s.op("pe", lambda e, kc=kc, b1=b1, j=j: e.matmul(b1.ap[0:nq, :], lhsT=hT.ap[:, kc, j * nq:(j + 1) * nq],
                                                                             rhs=w1.ap[:, kc, KB:KB + 512], start=(kc == 0), stop=(kc == 7)),
                                 reads=[w1, hT], writes=[b1])
                        so2 = ost[rr("ost", 2)]
                        s.op("act", lambda e, b1=b1, so2=so2: e.activation(out=so2.ap[0:nq, 0:512], in_=b1.ap[0:nq, :], func=AF.Copy),
                             reads=[b1], writes=[so2])
                        s.dma(mchan(), lambda e, so2=so2, dbk=dbk: e.dma_start(out=dbk, in_=so2.ap[0:nq, 0:512]), reads=[so2])
                    s.sub('bkout')
                    dak = outs["ak"](j)
                    if dak is not None:
                        b1 = pbank[rr("bank", 6)]
                        for kc in range(8):
                            s.op("pe", lambda e, kc=kc, b1=b1, j=j: e.matmul(b1.ap[0:nq, 0:128], lhsT=hT.ap[:, kc, j * nq:(j + 1) * nq],
                                                                             rhs=w1.ap[:, kc, KA:KA + 128], start=(kc == 0), stop=(kc == 7)),
                                 reads=[w1, hT], writes=[b1])
                        rtm = outs["ropetm"]
                        so3 = ost[rr("ost", 2)]
                        kv = v4(b1.ap[0:nq, 0:128], 2, 2)
                        t1 = v4(so3.ap[0:nq, 0:128], 2, 2)
                        t2 = v4(so3.ap[0:nq, 128:256], 2, 2)
                        o3 = v4(so3.ap[0:nq, 256:384], 2, 2)
                        cosb = rtm.ap[0:nq, 0:32].unsqueeze(1).unsqueeze(1).to_broadcast([nq, 2, 2, 32])
                        sinb = rtm.ap[0:nq, 32:64].unsqueeze(1).to_broadcast([nq, 2, 32])
                        s.op("dve", lambda e, kv=kv, t1=t1, cosb=cosb: e.tensor_tensor(out=t1, in0=kv, in1=cosb, op=ALU.mult), reads=[b1, rtm], writes=[so3])
                        for hf in range(2):
                            s.op("dve", lambda e, kv=kv, t2=t2, sinb=sinb, hf=hf: e.tensor_tensor(out=t2[:, :, hf, :], in0=kv[:, :, 1 - hf, :], in1=sinb, op=ALU.mult),
                                 reads=[b1, rtm], writes=[so3])
                        s.op("dve", lambda e, t1=t1, t2=t2, o3=o3: e.tensor_tensor(out=o3[:, :, 0, :], in0=t1[:, :, 0, :], in1=t2[:, :, 0, :], op=ALU.subtract),
                             reads=[so3], writes=[so3])
                        s.op("dve", lambda e, t1=t1, t2=t2, o3=o3: e.tensor_tensor(out=o3[:, :, 1, :], in0=t1[:, :, 1, :], in1=t2[:, :, 1, :], op=ALU.add),
                             reads=[so3], writes=[so3])
                        s.dma(mchan(), lambda e, so3=so3, dak=dak: e.dma_start(out=dak, in_=so3.ap[0:nq, 256:384]), reads=[so3])

            s.checkpoint()
            xr_of = {}

            def load_xres(j):
                xr = xres[rr("xres", 2)]
                xr_of[j] = xr
                s.dma(f"xres{cnt['xres'] % 2}", lambda e, xr=xr, j=j: e.dma_start(out=xr.ap[0:nq, :], in_=xsrc(j)), writes=[xr])

            wf_of = {}

            def load_wf(f):
                w = wfs[rr("wf", 2)]
                wf_of[f] = w
                s.dma(f"wf{cnt['wf'] % 2}", lambda e, w=w, f=f: e.dma_start(out=w.ap, in_=v3(x_wf[f, :, :], 24)), reads=[b_wf], writes=[w])

            for j in range(min(2, nsub)):
                load_xres(j)
            load_wf(0)
            load_wf(1)

            Opair = ppairs[2]
            Ov = Opair.ap.rearrange("p (g c) -> p g c", g=2)[:, :, 0:260].rearrange("p g (h d) -> p g h d", d=65)

            def attn_units(j):
                units = []
                blksB = blocksB(j)
                blksA = blocksA(j)
                for h in range(8):
                    units.append(("B", h, blksB))
                for hp in range(4):
                    units.append(("A", hp, blksA))
                return units

            slot_pairs = [ppairs[0], ppairs[1], ppairs[3]]
            LOOK = 2

            def emit_scores(u, j, slot):
                kind, hx, blks = u
                pp = slot_pairs[slot]
                if kind == "B":
                    ti, p0 = hx // 2, 64 * (hx % 2)
                    for bi, (rb, eb, nk) in enumerate(blks):
                        s.op("pe", lambda e, bi=bi, rb=rb, nk=nk, ti=ti, p0=p0, pp=pp: e.matmul(
                            pp.ap[0:nk, bi * 128:bi * 128 + nq], lhsT=kbT_ap[p0:p0 + 64, ti, rb * 128:rb * 128 + nk],
                            rhs=qbT.ap[p0:p0 + 64, ti, j * nq:(j + 1) * nq], start=True, stop=True),
                            reads=[ring[rb]["kb"], qbT], writes=[pp])
                else:
                    for hh in range(2):
                        h = 2 * hx + hh
                        g = h // 4
                        ti, p0 = h // 2, 64 * (h % 2)
                        for bi, (rb, eb, nk) in enumerate(blks):
                            s.op("pe", lambda e, bi=bi, rb=rb, nk=nk, ti=ti, p0=p0, pp=pp, hh=hh, g=g: e.matmul(
                                pp.ap[0:nk, hh * 512 + bi * 128:hh * 512 + bi * 128 + nq],
                                lhsT=kaT_ap[p0:p0 + 64, g, rb * 128:rb * 128 + nk],
                                rhs=qaT.ap[p0:p0 + 64, ti, j * nq:(j + 1) * nq], start=True, stop=True),
                                reads=[ring[rb]["ka"], qaT], writes=[pp])

            def emit_softmax_pv(u, j, slot):
                kind, hx, blks = u
                pp = slot_pairs[slot]
                nb = len(blks)
                nfull = sum(1 for b in blks if b[2] == 128)
                eb0 = blks[0][1]
                if kind == "B":
                    P = pB[rr("pb", 3)]
                    if nfull:
                        n0 = min(nfull, 4)
                        s.op("act", lambda e, P=P, pp=pp: e.activation(out=P.ap[:, 0:n0, 0:nq], in_=v3(pp.ap[:, 0:512], 4)[:, 0:n0, 0:nq],
                                                                       func=AF.Exp, scale=0.125), reads=[pp], writes=[P])
                        if nfull == 5:
                            s.op("act", lambda e, P=P, pp=pp: e.activation(out=P.ap[:, 4, 0:nq], in_=pp.ap[:, 512:512 + nq],
                                                                           func=AF.Exp, scale=0.125), reads=[pp], writes=[P])
                        s.op("dve", lambda e, P=P: e.tensor_tensor(out=P.ap[:, 0:nfull, 0:nq], in0=P.ap[:, 0:nfull, 0:nq],
                                                                    in1=e_b.ap[:, hx, eb0:eb0 + nfull, 0:nq], op=ALU.mult),
                             reads=[P, e_b], writes=[P])
                    for bi in range(nfull, nb):
                        rb, eb, nk = blks[bi]
                        s.op("act", lambda e, P=P, pp=pp, bi=bi, nk=nk: e.activation(out=P.ap[0:nk, bi, 0:nq], in_=pp.ap[0:nk, bi * 128:bi * 128 + nq],
                                                                                     func=AF.Exp, scale=0.125), reads=[pp], writes=[P])
                        s.op("dve", lambda e, P=P, bi=bi, nk=nk, eb=eb: e.tensor_tensor(out=P.ap[0:nk, bi, 0:nq], in0=P.ap[0:nk, bi, 0:nq],
                                                                                        in1=e_b.ap[0:nk, hx, eb, 0:nq], op=ALU.mult),
                             reads=[P, e_b], writes=[P])
                    h = hx
                    for bi, (rb, eb, nk) in enumerate(blks):
                        s.op("pe", lambda e, P=P, bi=bi, rb=rb, nk=nk, h=h: e.matmul(
                            Ov[0:nq, h // 4, h % 4, :], lhsT=P.ap[0:nk, bi, 0:nq], rhs=vb_ap[0:nk, rb, h, 0:65],
                            start=(bi == 0), stop=(bi == nb - 1)), reads=[P, ring[rb]["vb"]], writes=[Opair])
                else:
                    P = pA[rr("pa", 2)]
                    ppv = pp.ap.rearrange("p (h c) -> p h c", h=2)[:, :, 0:256].rearrange("p h (b q) -> p h b q", b=2)
                    if nfull:
                        for hh in range(2):
                            s.op("act", lambda e, P=P, ppv=ppv, hh=hh: e.activation(out=P.ap[:, hh, 0:nfull, 0:nq], in_=ppv[:, hh, 0:nfull, 0:nq],
                                                                                   func=AF.Exp, scale=0.125), reads=[pp], writes=[P])
                        s.op("dve", lambda e, P=P: e.tensor_tensor(out=P.ap[:, :, 0:nfull, 0:nq], in0=P.ap[:, :, 0:nfull, 0:nq],
                                                                    in1=e_a.ap[:, :, eb0:eb0 + nfull, 0:nq], op=ALU.mult),
                             reads=[P, e_a], writes=[P])
                    for bi in range(nfull, nb):
                        rb, eb, nk = blks[bi]
                        for hh in range(2):
                            s.op("act", lambda e, P=P, ppv=ppv, bi=bi, nk=nk, hh=hh: e.activation(out=P.ap[0:nk, hh, bi, 0:nq], in_=ppv[0:nk, hh, bi, 0:nq],
                                                                                                  func=AF.Exp, scale=0.125), reads=[pp], writes=[P])
                        s.op("dve", lambda e, P=P, bi=bi, nk=nk, eb=eb: e.tensor_tensor(out=P.ap[0:nk, :, bi, 0:nq], in0=P.ap[0:nk, :, bi, 0:nq],
                                                                                        in1=e_a.ap[0:nk, :, eb, 0:nq], op=ALU.mult),
                             reads=[P, e_a], writes=[P])
                    for hh in range(2):
                        h = 2 * hx + hh
                        g = h // 4
                        for bi, (rb, eb, nk) in enumerate(blks):
                            s.op("pe", lambda e, P=P, bi=bi, rb=rb, nk=nk, h=h, hh=hh, g=g: e.matmul(
                                Ov[0:nq, h // 4, h % 4, :], lhsT=P.ap[0:nk, hh, bi, 0:nq], rhs=va_ap[0:nk, rb, g, 0:65],
                                start=(bi == 0), stop=(bi == nb - 1)), reads=[P, ring[rb]["va"]], writes=[Opair])

            def emit_norm(kind, j, ot):
                rd = rden[rr("rden", 4)]
                c0 = 0 if kind == "A" else 512
                for g in range(2):
                    rdg = rd.ap[0:nq, g * 4:(g + 1) * 4]
                    den = Ov[0:nq, g, :, 64]
                    if kind == "A":
                        s.op("dve", lambda e, rdg=rdg, den=den, g=g: e.tensor_tensor(out=rdg, in0=den, in1=esink.ap[0:nq, g * 4:(g + 1) * 4], op=ALU.add),
                             reads=[Opair, esink], writes=[rd])
                        s.op("dve", lambda e, rdg=rdg: e.reciprocal(out=rdg, in_=rdg), reads=[rd], writes=[rd])
                    else:
                        s.op("dve", lambda e, rdg=rdg, den=den: e.reciprocal(out=rdg, in_=den), reads=[Opair], writes=[rd])
                    s.op("dve", lambda e, rdg=rdg, g=g: e.tensor_tensor(out=v3(ot.ap[0:nq, c0 + g * 256:c0 + (g + 1) * 256], 4), in0=Ov[0:nq, g, :, 0:64],
                                                                  in1=rdg.unsqueeze(2).to_broadcast([nq, 4, 64]), op=ALU.mult),
                         reads=[Opair, rd], writes=[ot])

            ot = otok[0]
            allunits = [(j, u) for j in range(nsub) for u in attn_units(j)]
            nun = len(allunits)

            def emit_tr(j):
                transpose_to(oT.ap[:, :, j * nq:(j + 1) * nq], ot, lambda k: ot.ap[0:nq, k * 128:(k + 1) * 128], nq, "dve", oT)

            pending_tr = None
            for k in range(min(LOOK, nun)):
                emit_scores(allunits[k][1], allunits[k][0], k % 3)
            for k, (j, u) in enumerate(allunits):
                if k + LOOK < nun:
                    emit_scores(allunits[k + LOOK][1], allunits[k + LOOK][0], (k + LOOK) % 3)
                emit_softmax_pv(u, j, k % 3)
                ui = k % 12
                if ui == 2 and pending_tr is not None:
                    emit_tr(pending_tr)
                    pending_tr = None
                if ui == 7:
                    emit_norm("B", j, ot)
                if ui == 11:
                    emit_norm("A", j, ot)
                    pending_tr = j
            if pending_tr is not None:
                emit_tr(pending_tr)

            s.checkpoint()
            for f in range(8):
                w = wf_of[f]
                bk = [pbank[4 * (f % 2) + i] for i in range(4)]
                for kc in range(4):
                    s.op("pe", lambda e, kc=kc, w=w, bk=bk: e.matmul(bk[0].ap[:, 0:Tt], lhsT=w.ap[:, kc, :], rhs=oT.ap[:, kc, 0:Tt],
                                                                     start=(kc == 0), stop=(kc == 3)), reads=[w, oT], writes=[bk[0]])
                for kc in range(4):
                    s.op("pe", lambda e, kc=kc, w=w, bk=bk: e.matmul(bk[1].ap[:, 0:Tt], lhsT=w.ap[:, 4 + kc, :], rhs=oT.ap[:, 4 + kc, 0:Tt],
                                                                     start=(kc == 0), stop=(kc == 3)), reads=[w, oT], writes=[bk[1]])
                for gi in range(2):
                    for kc in range(8):
                        s.op("pe", lambda e, kc=kc, w=w, bk=bk, gi=gi: e.matmul(bk[2 + gi].ap[:, 0:Tt], lhsT=w.ap[:, 8 + 8 * gi + kc, :],
                                                                                rhs=hT.ap[:, kc, 0:Tt], start=(kc == 0), stop=(kc == 7)),
                             reads=[w, hT], writes=[bk[2 + gi]])
                if f + 2 < 8:
                    load_wf(f + 2)
                for gi in range(2):
                    s.op("act", lambda e, gi=gi, bk=bk: e.activation(out=tab[gi].ap[:, 0:Tt], in_=bk[2 + gi].ap[:, 0:Tt], func=AF.Tanh, scale=0.5),
                         reads=[bk[2 + gi]], writes=[tab[gi]])
                    s.op("dve", lambda e, gi=gi, bk=bk: e.scalar_tensor_tensor(out=t12[gi].ap[:, 0:Tt], in0=tab[gi].ap[:, 0:Tt], scalar=1.0,
                                                                               in1=bk[gi].ap[:, 0:Tt], op0=ALU.add, op1=ALU.mult),
                         reads=[tab[gi], bk[gi]], writes=[t12[gi]])
                s.op("dve", lambda e, f=f: e.tensor_tensor(out=mT.ap[:, f, 0:Tt], in0=t12[0].ap[:, 0:Tt], in1=t12[1].ap[:, 0:Tt], op=ALU.add),
                     reads=[t12[0], t12[1]], writes=[mT])
            if pre_out is not None:
                pre_out()
            for j in range(nsub):
                zp = ppairs[j % 2]
                for hf in range(2):
                    for kc in range(8):
                        s.op("pe", lambda e, kc=kc, hf=hf, zp=zp, j=j: e.matmul(zp.ap[0:nq, hf * 512:(hf + 1) * 512],
                                                                                lhsT=mT.ap[:, kc, j * nq:(j + 1) * nq],
                                                                                rhs=wout.ap[:, kc, hf * 512:(hf + 1) * 512],
                                                                                start=(kc == 0), stop=(kc == 7)), reads=[mT, wout], writes=[zp])
                if mid_out is not None:
                    mid_out(j)
                ssT, rsT = new_ss()
                for hf in range(2):
                    s.op("act", lambda e, zp=zp, ssT=ssT, hf=hf: e.activation(out=junk[0:nq, hf * 512:(hf + 1) * 512], in_=zp.ap[0:nq, hf * 512:(hf + 1) * 512],
                                                                          func=AF.Square, accum_out=ssT.ap[0:nq, hf:hf + 1]), reads=[zp], writes=[ssT, junkT])
                emit_rstd(ssT, rsT, nq)
                xr = xr_of[j]
                for hf in range(2):
                    s.op("dve", lambda e, zp=zp, rsT=rsT, hf=hf: e.scalar_tensor_tensor(out=tmpz.ap[0:nq, hf * 512:(hf + 1) * 512], in0=zp.ap[0:nq, hf * 512:(hf + 1) * 512],
                                                                                   scalar=rsT.ap[0:nq], in1=gpost.ap[0:nq, hf * 512:(hf + 1) * 512],
                                                                                   op0=ALU.mult, op1=ALU.mult),
                         reads=[zp, rsT, gpost], writes=[tmpz])
                s.op("dve", lambda e, xr=xr: e.tensor_tensor(out=xr.ap[0:nq, :], in0=tmpz.ap[0:nq, :], in1=xr.ap[0:nq, :], op=ALU.add),
                     reads=[tmpz, xr], writes=[xr])
                s.dma(mchan(), lambda e, xr=xr, j=j: e.dma_start(out=x1dst(j), in_=xr.ap[0:nq, :]), reads=[xr], writes=[b_x1])
                if j + 2 < nsub:
                    load_xres(j + 2)
                if per_out is not None:
                    per_out(j)

        tiles1 = []
        if sample:
            cs = cstg
            cb = cstb
            s.dma(mchan(), lambda e: e.dma_start(out=v3(cs.ap, 4), in_=d_cbk.rearrange("(b p) c -> p b c", p=128)), writes=[cs])
            s.op("dve", lambda e: e.tensor_copy(out=cb.ap, in_=cs.ap), reads=[cs], writes=[cb])
            for blk in range(4):
                pt = pT[rr("pt", 2)]
                ptv = v3(pt.ap, 8)
                for ti in range(4):
                    s.op("pe", lambda e, blk=blk, ti=ti, ptv=ptv: e.transpose(out=ptv[:, ti, :], in_=cb.ap[:, blk * 512 + ti * 128: blk * 512 + (ti + 1) * 128],
                                                                            identity=ident.ap), reads=[cb, ident], writes=[pt])
                s.op("dve", lambda e, blk=blk, ptv=ptv: e.tensor_copy(out=kbT_ap[:, :, blk * 128:(blk + 1) * 128], in_=ptv[:, 0:4, :]),
                     reads=[pt], writes=[ring[blk]["kb"]])
            s.dma(mchan(), lambda e: e.dma_start(out=v3(cs.ap, 4), in_=d_cbv.rearrange("(b p) c -> p b c", p=128)), writes=[cs])
            s.op("dve", lambda e: e.tensor_copy(out=vb_ap[:, 0:4, :, 0:64], in_=v4(cs.ap, 4, 8)), reads=[cs],
                 writes=[ring[b]["vb"] for b in range(4)])
            s.dma(mchan(), lambda e: e.dma_start(out=cs.ap[:, 0:128], in_=d_cak[:, :]), writes=[cs])
            s.dma(mchan(), lambda e: e.dma_start(out=cs.ap[:, 128:256], in_=d_cav[:, :]), writes=[cs])
            s.op("dve", lambda e: e.tensor_copy(out=v4(cb.ap[:, 0:256], 2, 2), in_=v3(cs.ap[:, 0:128], 2).unsqueeze(2).to_broadcast([128, 2, 2, 64])),
                 reads=[cs], writes=[cb])
            pt = pT[rr("pt", 2)]
            ptv = v3(pt.ap, 8)
            for ti in range(2):
                s.op("pe", lambda e, ti=ti, ptv=ptv: e.transpose(out=ptv[:, ti, :], in_=cb.ap[:, ti * 128:(ti + 1) * 128], identity=ident.ap),
                     reads=[cb, ident], writes=[pt])
            s.op("dve", lambda e, ptv=ptv: e.tensor_copy(out=kaT_ap[:, :, 0:128], in_=ptv[:, 0:2, :]), reads=[pt], writes=[ring[0]["ka"]])
            s.op("dve", lambda e: e.tensor_copy(out=va_ap[:, 0, :, 0:64], in_=v3(cs.ap[:, 128:256], 2)), reads=[cs], writes=[ring[0]["va"]])
            s.checkpoint()
            rtm_s = T(A.f32(64))
            s.dma(mchan(), lambda e: e.dma_start(out=rtm_s.ap[0:16, :], in_=d_ropetm_s[:, :]), writes=[rtm_s])
            outs_s = dict(bv=lambda j: o_bvs[:, :], av=lambda j: o_avs[:, :], bk=lambda j: o_bks[:, :], ak=lambda j: o_aks[:, :], ropetm=rtm_s)
            tiles1.append(dict(xsrc=lambda j: d_xs[:, :], x1dst=lambda j: x_x1s[:, :], nsub=1, nq=16, rope_src=d_ropefm_s[:, :, :],
                               blocksA=lambda j: [(0, 0, 128), (1, 1, 16)],
                               blocksB=lambda j: [(0, 0, 128), (1, 1, 128), (2, 2, 128), (3, 3, 128), (4, 4, 16)],
                               newA=lambda j: 1, newB=lambda j: 4, outs=outs_s))
        else:
            rtm_s = T(A.f32(64))
        rtm_p = T(A.f32(64))
        s.dma(mchan(), lambda e: e.dma_start(out=rtm_p.ap, in_=d_ropetm[:, :]), writes=[rtm_p])

        s.checkpoint()
        ntile = S // T1
        for sq in range(NSEQ):
            for t in range(ntile):
                last = (t == ntile - 1)
                qb0 = 4 * t

                def blocksB(j, qb0=qb0):
                    qb = qb0 + j
                    return [(kb % 8, 4 - (qb - kb), 128) for kb in range(max(0, qb - 4), qb + 1)]

                def blocksA(j, qb0=qb0):
                    qb = qb0 + j
                    return [(kb % 8, 1 - (qb - kb), 128) for kb in range(max(0, qb - 1), qb + 1)]

                def newblk(j, qb0=qb0):
                    return (qb0 + j) % 8

                outs = None
                if last:
                    def rows_b(j, t=t):
                        return (t * T1 + j * 128) - (S - KB_ROWS)
                    outs = dict(
                        bv=lambda j, sq=sq, rows_b=rows_b: o_bvp[sq, rows_b(j):rows_b(j) + 128, :] if rows_b(j) >= 0 else None,
                        bk=lambda j, sq=sq, rows_b=rows_b: o_bkp[sq, rows_b(j):rows_b(j) + 128, :] if rows_b(j) >= 0 else None,
                        av=lambda j, sq=sq: o_avp[sq, :, :] if j == 3 else None,
                        ak=lambda j, sq=sq: o_akp[sq, :, :] if j == 3 else None,
                        ropetm=rtm_p)
                tiles1.append(dict(xsrc=lambda j, sq=sq, t=t: d_xp[sq, t * T1 + j * 128: t * T1 + (j + 1) * 128, :],
                                   x1dst=lambda j, sq=sq, t=t: x_x1p[sq, t * T1 + j * 128: t * T1 + (j + 1) * 128, :],
                                   nsub=4, nq=128, rope_src=d_ropefm[:, :, t * T1:(t + 1) * T1],
                                   blocksA=blocksA, blocksB=blocksB, newA=newblk, newB=newblk, outs=outs))

        p0s = [dict(xsrc=tk["xsrc"], nsub=tk["nsub"], nq=tk["nq"], xi={}, xb={}) for tk in tiles1]
        for j in range(min(2, p0s[0]["nsub"])):
            p1_load_x(p0s[0], j)
        for j in range(p0s[0]["nsub"]):
            p1_stage0a(p0s[0], j)
            p1_stage0b(p0s[0], j)
        for i, tk in enumerate(tiles1):
            nx = p0s[i + 1] if i + 1 < len(tiles1) else None
            if nx is not None:
                for j in range(min(2, nx["nsub"])):
                    p1_load_x(nx, j)

            def pre_out(nx=nx):
                if nx is not None:
                    p1_stage0a(nx, 0)

            def mid_out(j, nx=nx):
                if nx is not None and j + 1 < nx["nsub"]:
                    p1_stage0a(nx, j + 1)

            def per_out(j, nx=nx):
                if nx is not None and j < nx["nsub"]:
                    p1_stage0b(nx, j)

            p1_tile(pre_out=pre_out, per_out=per_out, mid_out=mid_out, **tk)
            if nx is not None:
                for j in range(tk["nsub"], nx["nsub"]):
                    p1_stage0b(nx, j)
                    if j + 1 < nx["nsub"]:
                        p1_stage0a(nx, j + 1)

        s.checkpoint()
        s.barrier()
        A.off = base_off
        wup = T(v3(A.bf16(8 * 2 * DFF), 8))
        wdn = T(v3(A.bf16(NFT * D), NFT))
        g2post = T(A.f32(D))
        g2pre = T(A.f32(8))
        cw = T(v3(A.f32(NFT * 3), NFT))
        cbias = T(A.f32(NFT))
        carry = T(v3(A.f32(NFT * 2), NFT))
        nodep.extend([wup, wdn])
        act2_off = A.off
        st2 = [T(A.f32(2816)) for _ in range(4)]
        s.dma(mchan(), lambda e: e.dma_start(out=g2pre.ap, in_=d_g2pre[:, :]), writes=[g2pre])
        s.dma(mchan(), lambda e: e.dma_start(out=g2post.ap, in_=d_g2post[0:1, :].partition_broadcast(128)), writes=[g2post])
        s.dma(mchan(), lambda e: e.dma_start(out=cw.ap, in_=v3(d_cw, NFT)), writes=[cw])
        s.dma(mchan(), lambda e: e.dma_start(out=cbias.ap, in_=d_cb[:, :]), writes=[cbias])
        i2 = 0
        for kc in range(8):
            for hf in range(2):
                stg = st2[i2 % 4]
                s.dma(f"st{i2 % 4}", lambda e, stg=stg, kc=kc, hf=hf: e.dma_start(out=stg.ap, in_=d_wup[kc * 128:(kc + 1) * 128, hf * DFF:(hf + 1) * DFF]),
                      writes=[stg])
                cvt(wup.ap[:, kc, hf * DFF:(hf + 1) * DFF], stg.ap, g2pre.ap[:, kc:kc + 1], [stg, g2pre], [wup])
                i2 += 1
        for kc in range(NFT):
            stg = st2[i2 % 4]
            s.dma(f"st{i2 % 4}", lambda e, stg=stg, kc=kc: e.dma_start(out=stg.ap[:, 0:D], in_=d_wdn[kc * 128:(kc + 1) * 128, :]), writes=[stg])
            cvt(wdn.ap[:, kc, :], stg.ap[:, 0:D], 1.0, [stg], [wdn])
            i2 += 1
        s.barrier()
        A.off = act2_off
        xin2 = [T(A.f32(D)) for _ in range(2)]
        xres2 = [T(A.f32(D)) for _ in range(2)]
        xsb2 = [T(A.bf16(D)) for _ in range(2)]
        h2T = T(v3(A.bf16(8 * T2), 8))
        gT = T(v3(A.bf16(NFT * T2), NFT))
        ubuf = [T(A.f32(T2 + 16)) for _ in range(2)]
        acc_off = A.off
        acc = [T(A.f32(T2)) for _ in range(2)]
        ge = [T(A.f32(T2)) for _ in range(2)]
        tmpz2 = T(arena_t[:, acc_off:acc_off + D], acc[0].bufs + acc[1].bufs)
        cvst = T(A.f32(512))
        assert A.off <= NW, A.off
        cnt2 = dict(xin=0, xres=0, ub=0, acc=0, ge=0, pt=0, zp=0, cv=0, xsb=0)

        def rr2(name, n):
            v = cnt2[name] % n
            cnt2[name] += 1
            return v

        class Tile2:
            def __init__(self, xsrc, ydst, nsub, nq, conv_dst, first):
                self.xsrc, self.ydst, self.nsub, self.nq, self.conv_dst, self.first = xsrc, ydst, nsub, nq, conv_dst, first
                self.Tt = nsub * nq
                self.xr_of = {}
                self.xi_of = {}
                self.xb_of = {}

        def p2_load_x(tl, j):
            xi = xin2[rr2("xin", 2)]
            tl.xi_of[j] = xi
            s.dma(f"xin{cnt2['xin'] % 2}", lambda e: e.dma_start(out=xi.ap[0:tl.nq, :], in_=tl.xsrc(j)), reads=[b_x1], writes=[xi])

        def p2_stage0a(tl, j):
            nq = tl.nq
            xi = tl.xi_of[j]
            xb = xsb2[rr2("xsb", 2)]
            tl.xb_of[j] = xb
            ssT, rsT = new_ss()
            jk = xb.ap
            s.op("act", lambda e: e.activation(out=jk[0:nq, :], in_=xi.ap[0:nq, :], func=AF.Square, accum_out=ssT.ap[0:nq, 0:1]),
                 reads=[xi], writes=[ssT, xb])
            emit_rstd(ssT, rsT, nq, ncols=1)
            s.op("dve", lambda e: e.tensor_scalar(out=xb.ap[0:nq, :], in0=xi.ap[0:nq, :], scalar1=rsT.ap[0:nq], scalar2=None, op0=ALU.mult),
                 reads=[xi, rsT], writes=[xb])
            if j + 2 < tl.nsub:
                p2_load_x(tl, j + 2)

        def p2_stage0b(tl, j):
            nq = tl.nq
            xb = tl.xb_of[j]
            pt = pT[rr2("pt", 2)]
            ptv = v3(pt.ap, 8)
            for k in range(8):
                s.op("pe", lambda e, k=k: e.transpose(out=ptv[:, k, 0:nq], in_=xb.ap[0:nq, k * 128:(k + 1) * 128], identity=ident.ap[0:nq, 0:nq]),
                     reads=[xb, ident], writes=[pt])
            s.op("act", lambda e: e.activation(out=h2T.ap[:, :, j * nq:(j + 1) * nq], in_=ptv[:, :, 0:nq], func=AF.Copy), reads=[pt], writes=[h2T])

        def p2_load_res(tl, j):
            xr = xres2[rr2("xres", 2)]
            tl.xr_of[j] = xr
            s.dma(f"xres{cnt2['xres'] % 2}", lambda e: e.dma_start(out=xr.ap[0:tl.nq, :], in_=tl.xsrc(j)), reads=[b_x1], writes=[xr])

        def p2_ffn(tl):
            Tt = tl.Tt
            if tl.first:
                if tl.first == "zero":
                    s.op("pool", lambda e: e.memset(carry.ap, 0.0), writes=[carry])
                else:
                    s.dma(mchan(), lambda e: e.dma_start(out=carry.ap, in_=v3(d_sconv, NFT)), writes=[carry])
            for j in range(min(2, tl.nsub)):
                p2_load_res(tl, j)
            pend_tail = [None]
            for f in range(NFT):
                bu = pbank[2 * (f % 4)]
                bv = pbank[2 * (f % 4) + 1]
                for kc in range(8):
                    s.op("pe", lambda e, kc=kc, f=f, bu=bu: e.matmul(bu.ap[:, 0:Tt], lhsT=wup.ap[:, kc, f * 128:(f + 1) * 128], rhs=h2T.ap[:, kc, 0:Tt],
                                                                     start=(kc == 0), stop=(kc == 7)), reads=[wup, h2T], writes=[bu])
                for kc in range(8):
                    s.op("pe", lambda e, kc=kc, f=f, bv=bv: e.matmul(bv.ap[:, 0:Tt], lhsT=wup.ap[:, kc, DFF + f * 128:DFF + (f + 1) * 128],
                                                                     rhs=h2T.ap[:, kc, 0:Tt], start=(kc == 0), stop=(kc == 7)),
                         reads=[wup, h2T], writes=[bv])
                ub = ubuf[rr2("ub", 2)]
                ac = acc[rr2("acc", 2)]
                gg = ge[rr2("ge", 2)]
                s.op("act", lambda e, ub=ub, f=f: e.activation(out=ub.ap[:, 0:2], in_=carry.ap[:, f, :], func=AF.Copy), reads=[carry], writes=[ub])
                s.op("act", lambda e, ub=ub, bu=bu: e.activation(out=ub.ap[:, 2:2 + Tt], in_=bu.ap[:, 0:Tt], func=AF.Copy), reads=[bu], writes=[ub])
                s.op("act", lambda e, bu=bu, f=f: e.activation(out=carry.ap[:, f, :], in_=bu.ap[:, Tt - 2:Tt], func=AF.Copy), reads=[bu], writes=[carry])
                s.op("act", lambda e, bu=bu, ac=ac, f=f: e.activation(out=ac.ap[:, 0:Tt], in_=bu.ap[:, 0:Tt], func=AF.Identity,
                                                                      bias=cbias.ap[:, f:f + 1], scale=cw.ap[:, f, 2:3]),
                     reads=[bu, cw, cbias], writes=[ac])
                s.op("dve", lambda e, ub=ub, ac=ac, f=f: e.scalar_tensor_tensor(out=ac.ap[:, 0:Tt], in0=ub.ap[:, 1:1 + Tt], scalar=cw.ap[:, f, 1:2],
                                                                                in1=ac.ap[:, 0:Tt], op0=ALU.mult, op1=ALU.add),
                     reads=[ub, cw, ac], writes=[ac])
                s.op("dve", lambda e, ub=ub, ac=ac, f=f: e.scalar_tensor_tensor(out=ac.ap[:, 0:Tt], in0=ub.ap[:, 0:Tt], scalar=cw.ap[:, f, 0:1],
                                                                                in1=ac.ap[:, 0:Tt], op0=ALU.mult, op1=ALU.add),
                     reads=[ub, cw, ac], writes=[ac])
                def tail(ac=ac, gg=gg, bv=bv, f=f):
                    s.op("act", lambda e: e.activation(out=gg.ap[:, 0:Tt], in_=ac.ap[:, 0:Tt], func=AF.Gelu), reads=[ac], writes=[gg])
                    s.op("dve", lambda e: e.tensor_tensor(out=gT.ap[:, f, 0:Tt], in0=gg.ap[:, 0:Tt], in1=bv.ap[:, 0:Tt], op=ALU.mult),
                         reads=[gg, bv], writes=[gT])
                if pend_tail[0] is not None:
                    pend_tail[0]()
                pend_tail[0] = tail
            pend_tail[0]()
            pend_tail[0] = None
            if tl.conv_dst is not None:
                for c0 in range(0, DFF, 512):
                    n = min(512, DFF - c0)
                    bk = pbank[4 + rr2("cv", 2)]
                    for kc in range(8):
                        s.op("pe", lambda e, kc=kc, c0=c0, n=n, bk=bk: e.matmul(bk.ap[0:2, 0:n], lhsT=h2T.ap[:, kc, Tt - 2:Tt], rhs=wup.ap[:, kc, c0:c0 + n],
                                                                                start=(kc == 0), stop=(kc == 7)), reads=[wup, h2T], writes=[bk])
                    s.op("act", lambda e, n=n, bk=bk: e.activation(out=cvst.ap[0:2, 0:n], in_=bk.ap[0:2, 0:n], func=AF.Copy), reads=[bk], writes=[cvst])
                    dst = tl.conv_dst[:, c0:c0 + n]
                    s.dma(mchan(), lambda e, n=n, dst=dst: e.dma_start(out=dst, in_=cvst.ap[0:2, 0:n]), reads=[cvst])

        def p2_down(tl, j, mid=None):
            nq = tl.nq
            zp = ppairs[rr2("zp", 2)]
            for hf in range(2):
                for kc in range(NFT):
                    s.op("pe", lambda e, kc=kc, hf=hf: e.matmul(zp.ap[0:nq, hf * 512:(hf + 1) * 512], lhsT=gT.ap[:, kc, j * nq:(j + 1) * nq],
                                                                rhs=wdn.ap[:, kc, hf * 512:(hf + 1) * 512],
                                                                start=(kc == 0), stop=(kc == NFT - 1)), reads=[gT, wdn], writes=[zp])
            if mid is not None:
                mid()
            ssT, rsT = new_ss()
            jk = tmpz2.ap.bitcast(BF16)
            for hf in range(2):
                s.op("act", lambda e, hf=hf: e.activation(out=jk[0:nq, hf * 512:(hf + 1) * 512], in_=zp.ap[0:nq, hf * 512:(hf + 1) * 512],
                                                          func=AF.Square, accum_out=ssT.ap[0:nq, hf:hf + 1]), reads=[zp], writes=[ssT, tmpz2])
            emit_rstd(ssT, rsT, nq)
            xr = tl.xr_of[j]
            for hf in range(2):
                s.op("dve", lambda e, hf=hf: e.scalar_tensor_tensor(out=tmpz2.ap[0:nq, hf * 512:(hf + 1) * 512], in0=zp.ap[0:nq, hf * 512:(hf + 1) * 512],
                                                                    scalar=rsT.ap[0:nq], in1=g2post.ap[0:nq, hf * 512:(hf + 1) * 512],
                                                                    op0=ALU.mult, op1=ALU.mult), reads=[zp, rsT, g2post], writes=[tmpz2])
            s.op("dve", lambda e: e.tensor_tensor(out=xr.ap[0:nq, :], in0=tmpz2.ap[0:nq, :], in1=xr.ap[0:nq, :], op=ALU.add),
                 reads=[tmpz2, xr], writes=[xr])
            s.dma(mchan(), lambda e: e.dma_start(out=tl.ydst(j), in_=xr.ap[0:nq, :]), reads=[xr])
            if j + 2 < tl.nsub:
                p2_load_res(tl, j + 2)

        tiles2 = []
        if sample:
            tiles2.append(Tile2(lambda j: x_x1s[:, :], lambda j: o_ys[:, :], 1, 16, o_convs, "state"))
        ntile2 = S // T2
        for sq in range(NSEQ):
            for t in range(ntile2):
                tiles2.append(Tile2(lambda j, sq=sq, t=t: x_x1p[sq, t * T2 + j * 128: t * T2 + (j + 1) * 128, :],
                                    lambda j, sq=sq, t=t: o_yp[sq, t * T2 + j * 128: t * T2 + (j + 1) * 128, :],
                                    T2 // 128, 128, (o_convp[sq] if t == ntile2 - 1 else None), ("zero" if t == 0 else None)))
        tl0 = tiles2[0]
        for j in range(min(2, tl0.nsub)):
            p2_load_x(tl0, j)
        for j in range(tl0.nsub):
            p2_stage0a(tl0, j)
            p2_stage0b(tl0, j)
        for i, tl in enumerate(tiles2):
            nxt = tiles2[i + 1] if i + 1 < len(tiles2) else None
            if nxt is not None:
                for j in range(min(2, nxt.nsub)):
                    p2_load_x(nxt, j)
            p2_ffn(tl)
            if nxt is not None:
                p2_stage0a(nxt, 0)
            for j in range(tl.nsub):
                def mid(j=j, nxt=nxt):
                    if nxt is not None and j + 1 < nxt.nsub:
                        p2_stage0a(nxt, j + 1)
                p2_down(tl, j, mid)
                if nxt is not None and j < nxt.nsub:
                    p2_stage0b(nxt, j)
            if nxt is not None:
                for j in range(tl.nsub, nxt.nsub):
                    p2_stage0b(nxt, j)
                    if j + 1 < nxt.nsub:
                        p2_stage0a(nxt, j + 1)

        s.final_wait()
        s.emit(nc, stack)
    return nc


def _consts(S):
    bf = ml_dtypes.bfloat16
    half = 32
    inv = 1.0 / (10000.0 ** (np.arange(half, dtype=np.float32) * np.float32(2.0 / 64)))
    inv = inv.astype(np.float32)

    def fm(pos):
        ang = pos.astype(np.float32)[None, :] * inv[np.arange(128) % 32][:, None]
        return np.stack([np.cos(ang), np.sin(ang)], axis=1).astype(np.float32)

    def tm(pos):
        ang = pos.astype(np.float32)[:, None] * inv[None, :]
        return np.concatenate([np.cos(ang), np.sin(ang)], axis=1).astype(np.float32)

    c = {}
    c["ropefm"] = np.ascontiguousarray(fm(np.arange(S)))
    c["ropefm_s"] = np.ascontiguousarray(fm(1024 + np.arange(16)))
    c["ropetm"] = np.ascontiguousarray(tm(np.arange(S - 128, S)))
    c["ropetm_s"] = np.ascontiguousarray(tm(1024 + np.arange(16)))
    c["ident"] = np.eye(128, dtype=np.float32).astype(bf)
    c["jmat"] = np.ascontiguousarray(np.eye(128, dtype=np.float32)[::-1]).astype(bf)
    perms = np.zeros((128, 5, 128), dtype=np.float32)
    for m in range(128):
        d = m % 64
        base = m - d
        if d < 32:
            perms[base + d + 32, 0, m] = -1.0
        else:
            perms[base + d - 32, 0, m] = 1.0
        for g in range(2):
            perms[g * 64 + d, 1 + g, m] = 1.0
            if d < 32:
                perms[g * 64 + d + 32, 3 + g, m] = -1.0
            else:
                perms[g * 64 + d - 32, 3 + g, m] = 1.0
    c["perms"] = perms.reshape(128, 5 * 128).astype(bf)
    ea = np.ones((128, 2, 2, 128), dtype=np.float32)
    ea[0:64, :, 0, 64:128] = 0.0
    ea[64:128, :, 1, 0:64] = 0.0
    c["ea"] = ea.reshape(128, 512).astype(bf)
    return c


_NC_CACHE = {}


def _run(inputs, S, NSEQ_TOTAL, n_cores, sample=True):
    nseq = NSEQ_TOTAL // n_cores
    key = (S, nseq, sample)
    if key not in _NC_CACHE:
        _NC_CACHE[key] = build_nc(S, nseq, sample)
    nc = _NC_CACHE[key]
    f = lambda a: np.ascontiguousarray(np.asarray(a, dtype=np.float32))
    consts = _consts(S)
    shared = dict(
        w_in=f(inputs["w_in"][0]), w_oa=f(inputs["w_oa"][0]), w_ob=f(inputs["w_ob"][0]), w_out=f(inputs["w_out"][0]),
        w_up=f(inputs["w_up"][0]), w_down=f(inputs["w_down"][0]),
        gpre=f(np.asarray(inputs["g_mix_pre"][0]).reshape(8, 128).T), gpost=f(np.asarray(inputs["g_mix_post"][0]).reshape(1, D)),
        g2pre=f(np.asarray(inputs["g_ffn_pre"][0]).reshape(8, 128).T), g2post=f(np.asarray(inputs["g_ffn_post"][0]).reshape(1, D)),
        sinks=f(np.asarray(inputs["sinks"][0]).reshape(1, 8)),
        trev=f(np.asarray(inputs["rel_bias"][0])[:, ::-1]),
        cw=f(np.asarray(inputs["conv_w"][0]).reshape(3, NFT, 128).transpose(2, 1, 0).reshape(128, NFT * 3)),
        cb=f(np.asarray(inputs["conv_b"][0]).reshape(NFT, 128).T),
        **consts,
    )
    xp = np.asarray(inputs["x_prompt"], dtype=np.float32)
    in_maps = []
    for c in range(n_cores):
        m = dict(shared)
        m["xp"] = np.ascontiguousarray(xp[c * nseq:(c + 1) * nseq])
        m["xs"] = f(inputs["x_sample"][c])
        m["cak"] = f(np.asarray(inputs["cache_a_k"][0, c]).reshape(128, 128))
        m["cav"] = f(np.asarray(inputs["cache_a_v"][0, c]).reshape(128, 128))
        m["cbk"] = f(np.asarray(inputs["cache_b_k"][0, c]).reshape(512, 512))
        m["cbv"] = f(np.asarray(inputs["cache_b_v"][0, c]).reshape(512, 512))
        m["sconv"] = f(np.asarray(inputs["state_conv"][0, c]).reshape(2, NFT, 128).transpose(2, 1, 0).reshape(128, NFT * 2))
        in_maps.append(m)
    res = run_bass_kernel_spmd(nc, in_maps, core_ids=list(range(n_cores)))
    R = res.results
    cat = lambda k: np.concatenate([np.asarray(r[k], dtype=np.float32) for r in R], axis=0)
    stk = lambda k: np.stack([np.asarray(r[k], dtype=np.float32) for r in R], axis=0)
    KA_ROWS = min(128, S)
    KB_ROWS = min(512, S)
    B = NSEQ_TOTAL
    return (
        cat("yp"),
        stk("ys"),
        cat("akp").reshape(1, B, KA_ROWS, 2, 64),
        cat("avp").reshape(1, B, KA_ROWS, 2, 64),
        cat("bkp").reshape(1, B, KB_ROWS, 8, 64),
        cat("bvp").reshape(1, B, KB_ROWS, 8, 64),
        cat("convp").reshape(1, B, 2, DFF),
        stk("aks").reshape(1, n_cores, 16, 2, 64),
        stk("avs").reshape(1, n_cores, 16, 2, 64),
        stk("bks").reshape(1, n_cores, 16, 8, 64),
        stk("bvs").reshape(1, n_cores, 16, 8, 64),
        stk("convs").reshape(1, n_cores, 2, DFF),
    )


def kernel(**inputs):
    xp = inputs["x_prompt"]
    return _run(inputs, S=int(xp.shape[1]), NSEQ_TOTAL=int(xp.shape[0]), n_cores=8, sample=True)
```

```python
from contextlib import ExitStack

import numpy as np
import ml_dtypes

import concourse.bass as bass
import concourse.mybir as mybir
from concourse.bass_utils import run_bass_kernel_spmd

F32 = mybir.dt.float32
BF16 = mybir.dt.bfloat16
AF = mybir.ActivationFunctionType
ALU = mybir.AluOpType

D = 1024
DFF = 2816
NFT = 22
EPS = 1e-6
T1 = 512
T2 = 512
NC1 = 2304
QA, KA, QB, KB, VB, VA = 0, 512, 640, 1152, 1664, 2176
ROT = 8000
ENG = ("pe", "act", "dve", "pool", "sp")


class Buf:
    __slots__ = ("w", "r", "excl")

    def __init__(self, excl=False):
        self.w = None
        self.r = {}
        self.excl = excl


class T:
    def __init__(self, ap, bufs=None):
        self.ap = ap
        self.bufs = bufs if bufs is not None else [Buf()]


def _bufs(items):
    out = []
    for it in items:
        if isinstance(it, T):
            out.extend(it.bufs)
        elif isinstance(it, Buf):
            out.append(it)
        else:
            out.extend(_bufs(it))
    return out


class Sched:
    def __init__(self):
        self.streams = {e: [] for e in ENG}
        self.cnt = {e: 0 for e in ENG}
        self.seen = {e: {} for e in ENG}
        self.chan = {}
        self.dead = False
        self.ckpt = 0
        import os as _os
        self.limit = int(_os.environ.get("KLIMIT", "0"))

    def sub(self, name):
        import os as _os
        if _os.environ.get("KSUB", "") == name:
            self.dead = True

    def checkpoint(self):
        self.ckpt += 1
        if self.limit and self.ckpt >= self.limit:
            self.dead = True

    def _waits(self, eng, reads, writes):
        d = {}
        for b in reads:
            if b.w is not None and b.w[1] > d.get(b.w[0], 0):
                d[b.w[0]] = b.w[1]
            if b.excl:
                for k, v in b.r.items():
                    if k != ("e", eng) and v > d.get(k, 0):
                        d[k] = v
        for b in writes:
            if b.w is not None and b.w[1] > d.get(b.w[0], 0):
                d[b.w[0]] = b.w[1]
            for k, v in b.r.items():
                if v > d.get(k, 0):
                    d[k] = v
        seen = self.seen[eng]
        waits = []
        for k, v in d.items():
            if eng == "pe" and k == ("e", "pe"):
                continue
            if v > seen.get(k, 0):
                waits.append((k, v))
                seen[k] = v
        return waits

    def _mark(self, key, val, reads, writes):
        for b in writes:
            b.w = (key, val)
            b.r = {}
        for b in reads:
            if b in writes:
                continue
            if val > b.r.get(key, 0):
                b.r[key] = val

    def op(self, eng, fn, reads=(), writes=()):
        if self.dead:
            return 0
        reads = _bufs(reads)
        writes = _bufs(writes)
        waits = self._waits(eng, reads, writes)
        self.cnt[eng] += 1
        idx = self.cnt[eng]
        self.streams[eng].append(("op", waits, fn, None))
        self._mark(("e", eng), idx, reads, writes)
        return idx

    def dma(self, chan, fn, reads=(), writes=(), queue="sp"):
        if self.dead:
            return
        reads = _bufs(reads)
        writes = _bufs(writes)
        waits = self._waits(queue, reads, writes)
        prev = self.chan.get(chan, 0)
        key = ("d", chan)
        if prev > self.seen[queue].get(key, 0):
            waits.append((key, prev))
            self.seen[queue][key] = prev
        self.chan[chan] = prev + 1
        self.streams[queue].append(("dma", waits, fn, chan))
        self._mark(key, prev + 1, reads, writes)

    def barrier(self):
        if self.dead:
            return
        for e in ENG:
            waits = []
            for e2 in ENG:
                if e2 == "sp":
                    continue
                v = self.cnt[e2]
                if v > self.seen[e].get(("e", e2), 0) and not (e == "pe" and e2 == "pe"):
                    waits.append((("e", e2), v))
                    self.seen[e][("e", e2)] = v
            for c, v in self.chan.items():
                if v > self.seen[e].get(("d", c), 0):
                    waits.append((("d", c), v))
                    self.seen[e][("d", c)] = v
            if waits:
                self.streams[e].append(("wait", waits, None, None))

    def final_wait(self):
        waits = []
        for c, v in self.chan.items():
            waits.append((("d", c), v))
        for e2 in ENG:
            if e2 != "sp" and self.cnt[e2] > 0:
                waits.append((("e", e2), self.cnt[e2]))
        self.streams["sp"].append(("wait", waits, None, None))

    def emit(self, nc, stack):
        sems = {}
        for e in ENG:
            if e == "sp":
                continue
            n = (self.cnt[e] + ROT - 1) // ROT
            for k in range(max(n, 1)):
                sems[("e", e, k)] = stack.enter_context(nc.semaphore(f"s_{e}_{k}"))
        for c in self.chan:
            sems[("d", c)] = stack.enter_context(nc.semaphore(f"d_{c}"))

        def do_wait(eobj, k, v):
            if k[0] == "e":
                kk = (v - 1) // ROT
                eobj.wait_ge(sems[("e", k[1], kk)], (v - 1) % ROT + 1)
            else:
                eobj.wait_ge(sems[("d", k[1])], 16 * v)

        def replay(ename, eobj):
            idx = 0
            for kind, waits, fn, chan in self.streams[ename]:
                for k, v in waits:
                    do_wait(eobj, k, v)
                if kind == "op":
                    idx += 1
                    ins = fn(eobj)
                    ins.then_inc(sems[("e", ename, (idx - 1) // ROT)], 1)
                elif kind == "dma":
                    ins = fn(eobj)
                    ins.then_inc(sems[("d", chan)], 16)

        block = stack.enter_context(nc.Block())

        @block.tensor
        def _(e):
            replay("pe", e)

        @block.scalar
        def _(e):
            replay("act", e)

        @block.vector
        def _(e):
            replay("dve", e)

        @block.gpsimd
        def _(e):
            replay("pool", e)

        @block.sync
        def _(e):
            replay("sp", e)


class Arena:
    def __init__(self, ap, nwords):
        self.ap = ap
        self.n = nwords
        self.off = 0

    def f32(self, n, parts=128):
        self.off = (self.off + 15) // 16 * 16
        assert self.off + n <= self.n, ("arena overflow", self.off + n, self.n)
        r = self.ap[0:parts, self.off:self.off + n]
        self.off += n
        return r

    def bf16(self, n, parts=128):
        assert n % 2 == 0
        return self.f32(n // 2, parts).bitcast(BF16)


def v3(ap, a):
    return ap.rearrange("p (a b) -> p a b", a=a)


def v4(ap, a, b):
    return ap.rearrange("p (a b c) -> p a b c", a=a, b=b)


def build_nc(S, NSEQ, sample=True):
    assert S % T1 == 0
    nc = bass.Bass("TRN2", target_bir_lowering=False)
    s = Sched()

    def din(name, shape, dt=F32):
        return nc.dram_tensor(name, list(shape), dt, kind="ExternalInput").ap()

    def dout(name, shape):
        return nc.dram_tensor(name, list(shape), F32, kind="ExternalOutput").ap()

    def dint(name, shape, dt=F32):
        return nc.dram_tensor(name, list(shape), dt).ap()

    KA_ROWS = min(128, S)
    KB_ROWS = min(512, S)
    d_xp = din("xp", [NSEQ, S, D])
    d_xs = din("xs", [16, D])
    d_cak = din("cak", [128, 128])
    d_cav = din("cav", [128, 128])
    d_cbk = din("cbk", [512, 512])
    d_cbv = din("cbv", [512, 512])
    d_sconv = din("sconv", [128, NFT * 2])
    d_win = din("w_in", [D, 4352])
    d_woa = din("w_oa", [512, D])
    d_wob = din("w_ob", [512, D])
    d_wout = din("w_out", [D, D])
    d_wup = din("w_up", [D, 2 * DFF])
    d_wdn = din("w_down", [DFF, D])
    d_gpre = din("gpre", [128, 8])
    d_gpost = din("gpost", [1, D])
    d_g2pre = din("g2pre", [128, 8])
    d_g2post = din("g2post", [1, D])
    d_sinks = din("sinks", [1, 8])
    d_trev = din("trev", [8, 257])
    d_cw = din("cw", [128, NFT * 3])
    d_cb = din("cb", [128, NFT])
    d_ident = din("ident", [128, 128], BF16)
    d_jmat = din("jmat", [128, 128], BF16)
    d_perms = din("perms", [128, 5 * 128], BF16)
    d_ea = din("ea", [128, 512], BF16)
    d_ropefm = din("ropefm", [128, 2, S])
    d_ropefm_s = din("ropefm_s", [128, 2, 16])
    d_ropetm = din("ropetm", [128, 64])
    d_ropetm_s = din("ropetm_s", [16, 64])
    o_yp = dout("yp", [NSEQ, S, D])
    o_ys = dout("ys", [16, D])
    o_akp = dout("akp", [NSEQ, KA_ROWS, 128])
    o_avp = dout("avp", [NSEQ, KA_ROWS, 128])
    o_bkp = dout("bkp", [NSEQ, KB_ROWS, 512])
    o_bvp = dout("bvp", [NSEQ, KB_ROWS, 512])
    o_convp = dout("convp", [NSEQ, 2, DFF])
    o_aks = dout("aks", [16, 128])
    o_avs = dout("avs", [16, 128])
    o_bks = dout("bks", [16, 512])
    o_bvs = dout("bvs", [16, 512])
    o_convs = dout("convs", [2, DFF])
    x_x1p = dint("x1p", [NSEQ, S, D])
    x_x1s = dint("x1s", [16, D])
    x_wf = dint("wf", [8, 128, 3072], BF16)
    x_ptab = dint("ptab", [8, 768])
    b_x1 = Buf()
    b_wf = Buf()
    b_ptab = Buf()

    stack = ExitStack()
    with stack:
        NW = 53200
        arena_t = stack.enter_context(nc.sbuf_tensor("arena", [128, NW], F32))
        psum_t = stack.enter_context(nc.psum_tensor("ps", [128, 4096], F32))
        A = Arena(arena_t, NW)
        pbank = [T(psum_t[:, 512 * i:512 * (i + 1)], [Buf(excl=True)]) for i in range(8)]

        def ppair(i):
            return T(psum_t[:, 1024 * i:1024 * (i + 1)], pbank[2 * i].bufs + pbank[2 * i + 1].bufs)

        ppairs = [ppair(i) for i in range(4)]
        pT = [T(psum_t[:, 512 * i:512 * (i + 1)].bitcast(BF16), pbank[i].bufs) for i in (6, 7)]

        ident = T(A.bf16(128))
        jmat = T(A.bf16(128))
        perms = T(v3(A.bf16(5 * 128), 5))
        NSS = 544
        ss_ap = A.f32(NSS)
        rs_ap = A.f32(NSS)
        mhalf = T(A.f32(1))
        ss_i = [0]

        def new_ss():
            i = ss_i[0]
            ss_i[0] += 2
            assert i + 1 < NSS
            return T(ss_ap[:, i:i + 2]), T(rs_ap[:, i:i + 1])

        s.dma("c0", lambda e: e.dma_start(out=ident.ap, in_=d_ident[:, :]), writes=[ident])
        s.dma("c1", lambda e: e.dma_start(out=jmat.ap, in_=d_jmat[:, :]), writes=[jmat])
        s.dma("c2", lambda e: e.dma_start(out=perms.ap, in_=v3(d_perms, 5)), writes=[perms])
        ssall = T(ss_ap)
        s.op("pool", lambda e: e.memset(ss_ap, 0.0), writes=[ssall])
        s.op("pool", lambda e: e.memset(mhalf.ap, -0.5), writes=[mhalf])
        base_off = A.off

        misc_i = [0]

        def mchan():
            misc_i[0] += 1
            return f"m{misc_i[0] % 6}"

        def emit_rstd(ssT, rsT, n, ncols=2):
            if ncols == 2:
                s.op("dve", lambda e: e.tensor_tensor(out=rsT.ap[0:n], in0=ssT.ap[0:n, 0:1], in1=ssT.ap[0:n, 1:2], op=ALU.add),
                     reads=[ssT], writes=[rsT])
                src = rsT
            else:
                src = ssT
            s.op("dve", lambda e: e.tensor_scalar(out=rsT.ap[0:n], in0=src.ap[0:n, 0:1], scalar1=1.0 / D, scalar2=EPS,
                                                  op0=ALU.mult, op1=ALU.add), reads=[src], writes=[rsT])
            s.op("pool", lambda e: e.tensor_tensor(out=rsT.ap[0:n], in0=rsT.ap[0:n], in1=mhalf.ap[0:n], op=ALU.pow),
                 reads=[rsT, mhalf], writes=[rsT])

        w1 = T(v3(A.bf16(8 * NC1), 8))
        wout = T(v3(A.bf16(8 * D), 8))
        gpost = T(A.f32(D))
        e_b = T(v4(A.bf16(8 * 5 * 128), 8, 5))
        e_a = T(v4(A.bf16(512), 2, 2))
        esink = T(A.f32(8))
        gpre = T(A.f32(8))
        ngpre = T(A.f32(8))
        act_off = A.off

        st = [T(A.f32(4352)) for _ in range(4)]
        tb16 = [T(A.bf16(2048)) for _ in range(2)]
        trev_sb = T(A.f32(257, parts=8))
        ptab_sb = T(A.f32(768, parts=8))
        epre = T(A.f32(640))
        eexp = T(A.bf16(640))
        assert A.off <= NW

        s.dma(mchan(), lambda e: e.dma_start(out=gpre.ap, in_=d_gpre[:, :]), writes=[gpre])
        s.dma(mchan(), lambda e: e.dma_start(out=gpost.ap, in_=d_gpost[0:1, :].partition_broadcast(128)), writes=[gpost])
        s.dma(mchan(), lambda e: e.dma_start(out=esink.ap, in_=d_sinks[0:1, :].partition_broadcast(128)), writes=[esink])
        s.dma(mchan(), lambda e: e.dma_start(out=e_a.ap, in_=v4(d_ea, 2, 2)), writes=[e_a])
        s.dma(mchan(), lambda e: e.dma_start(out=trev_sb.ap, in_=d_trev[:, :]), writes=[trev_sb])
        s.op("dve", lambda e: e.tensor_scalar(out=ngpre.ap, in0=gpre.ap, scalar1=-1.0, scalar2=None, op0=ALU.mult),
             reads=[gpre], writes=[ngpre])
        s.op("act", lambda e: e.activation(out=esink.ap, in_=esink.ap, func=AF.Exp), reads=[esink], writes=[esink])
        s.op("dve", lambda e: e.tensor_copy(out=ptab_sb.ap[:, 0:256], in_=trev_sb.ap[:, 1:257]), reads=[trev_sb], writes=[ptab_sb])
        s.op("dve", lambda e: e.tensor_copy(out=ptab_sb.ap[:, 256:768], in_=trev_sb.ap[:, 256:257].to_broadcast([8, 512])),
             reads=[trev_sb], writes=[ptab_sb])
        s.dma(mchan(), lambda e: e.dma_start(out=x_ptab[:, :], in_=ptab_sb.ap), reads=[ptab_sb], writes=[b_ptab])

        s.checkpoint()
        cv_i = [0]
        nodep = [w1, wout]

        def cvt(out_ap, in_ap, scal, reads, writes):
            writes = [w for w in writes if w not in nodep]
            cv_i[0] += 1
            if cv_i[0] % 2:
                s.op("dve", lambda e: e.tensor_scalar(out=out_ap, in0=in_ap, scalar1=scal, scalar2=None, op0=ALU.mult),
                     reads=reads, writes=writes)
            else:
                s.op("act", lambda e: e.activation(out=out_ap, in_=in_ap, func=AF.Copy, scale=scal), reads=reads, writes=writes)

        for h in range(8):
            src = bass.AP(x_ptab.tensor, h * 768, [[1, 128], [128, 5], [1, 128]])
            s.dma(mchan(), lambda e, src=src: e.dma_start(out=v3(epre.ap, 5), in_=src), reads=[b_ptab], writes=[epre])
            s.op("act", lambda e: e.activation(out=eexp.ap, in_=epre.ap, func=AF.Exp), reads=[epre], writes=[eexp])
            pp = ppairs[h % 2]
            s.op("pe", lambda e, pp=pp: e.matmul(pp.ap[:, 0:512], lhsT=jmat.ap, rhs=eexp.ap[:, 0:512], start=True, stop=True),
                 reads=[jmat, eexp], writes=[pp])
            s.op("pe", lambda e, pp=pp: e.matmul(pp.ap[:, 512:640], lhsT=jmat.ap, rhs=eexp.ap[:, 512:640], start=True, stop=True),
                 reads=[jmat, eexp], writes=[pp])
            for eb in range(5):
                s.op("dve", lambda e, pp=pp, h=h, eb=eb: e.tensor_copy(out=e_b.ap[:, h, eb, :], in_=pp.ap[:, (4 - eb) * 128:(5 - eb) * 128]),
                     reads=[pp], writes=[e_b])
        s.op("pool", lambda e: e.memset(e_b.ap[0:64, :, 0, 64:128], 0.0), writes=[e_b])
        s.op("pool", lambda e: e.memset(e_b.ap[64:128, :, 4, 0:64], 0.0), writes=[e_b])
        for kc in range(8):
            stg = st[kc % 4]
            tbb = tb16[kc % 2]
            s.dma(f"st{kc % 4}", lambda e, stg=stg, kc=kc: e.dma_start(out=stg.ap, in_=d_win[kc * 128:(kc + 1) * 128, :]),
                  writes=[stg])
            g = gpre.ap[:, kc:kc + 1]
            ng = ngpre.ap[:, kc:kc + 1]
            sa = stg.ap
            wk = w1.ap[:, kc, :]
            rd = [stg, gpre, ngpre]
            cvt(wk[:, QA:QA + 512], sa[:, 0:512], g, rd, [w1])
            cvt(wk[:, KA:KA + 128], sa[:, 512:640], g, rd, [w1])
            cvt(wk[:, QB:QB + 512], sa[:, 768:1280], g, rd, [w1])
            cvt(wk[:, KB:KB + 512], sa[:, 1280:1792], g, rd, [w1])
            cvt(wk[:, VB:VB + 512], sa[:, 1792:2304], g, rd, [w1])
            cvt(wk[:, VA:VA + 128], sa[:, 640:768], g, rd, [w1])
            cvt(tbb.ap, sa[:, 2304:4352], g, rd, [tbb])
            for gi in range(2):
                dst = bass.AP(x_wf.tensor, (8 + 8 * gi + kc) * 128, [[3072, 128], [128 * 3072, 8], [1, 128]])
                src = v3(tbb.ap[:, gi * 1024:(gi + 1) * 1024], 8)
                s.dma(mchan(), lambda e, dst=dst, src=src: e.dma_start(out=dst, in_=src), reads=[tbb], writes=[b_wf])
        s.checkpoint()
        for wi, dw in enumerate((d_woa, d_wob)):
            for kc in range(4):
                i = wi * 4 + kc
                stg = st[i % 4]
                tbb = tb16[i % 2]
                s.dma(f"st{i % 4}", lambda e, stg=stg, dw=dw, kc=kc: e.dma_start(out=stg.ap[:, 0:D], in_=dw[kc * 128:(kc + 1) * 128, :]),
                      writes=[stg])
                cvt(tbb.ap[:, 0:D], stg.ap[:, 0:D], 1.0, [stg], [tbb])
                dst = bass.AP(x_wf.tensor, (wi * 4 + kc) * 128, [[3072, 128], [128 * 3072, 8], [1, 128]])
                src = v3(tbb.ap[:, 0:D], 8)
                s.dma(mchan(), lambda e, dst=dst, src=src: e.dma_start(out=dst, in_=src), reads=[tbb], writes=[b_wf])
        for kc in range(8):
            stg = st[kc % 4]
            s.dma(f"st{kc % 4}", lambda e, stg=stg, kc=kc: e.dma_start(out=stg.ap[:, 0:D], in_=d_wout[kc * 128:(kc + 1) * 128, :]),
                  writes=[stg])
            cvt(wout.ap[:, kc, :], stg.ap[:, 0:D], 0.5, [stg], [wout])
        s.checkpoint()
        s.checkpoint()
        s.barrier()

        A.off = act_off
        xin = [T(A.f32(D)) for _ in range(2)]
        xres_off = A.off
        xres = [T(A.f32(D)) for _ in range(2)]
        cstg = T(arena_t[:, xres_off:xres_off + 2048], xres[0].bufs + xres[1].bufs)
        cstb = T(xin[0].ap.bitcast(BF16), xin[0].bufs)
        junkT = T(A.bf16(D))
        junk = junkT.ap
        xsb = [T(A.bf16(D)) for _ in range(2)]
        hT = T(v3(A.bf16(8 * T1), 8))
        ropeT = T(v3(A.f32(2 * T1), 2))
        qaT = T(v3(A.bf16(4 * T1), 4))
        qbT = T(v3(A.bf16(4 * T1), 4))
        kaT_ap = v3(A.bf16(2 * 1024), 2)
        kbT_ap = v3(A.bf16(4 * 1024), 4)
        va_ap = v4(A.bf16(8 * 2 * 66), 8, 2)
        vb_ap = v4(A.bf16(8 * 8 * 66), 8, 8)
        ring = [dict(ka=Buf(), kb=Buf(), va=Buf(), vb=Buf()) for _ in range(8)]
        qraw = [T(A.bf16(T1)) for _ in range(2)]
        rtmp_off = A.off
        rtmp = [T(A.f32(T1)) for _ in range(2)]
        tmpz = T(arena_t[:, rtmp_off:rtmp_off + D], rtmp[0].bufs + rtmp[1].bufs)
        pB = [T(v3(A.bf16(640), 5)) for _ in range(3)]
        pA = [T(v4(A.bf16(512), 2, 2)) for _ in range(2)]
        otok = [T(A.bf16(D)) for _ in range(1)]
        rden = [T(A.f32(8)) for _ in range(4)]
        oT = T(v3(A.bf16(8 * T1), 8))
        tab = [T(A.f32(T1)) for _ in range(2)]
        t12 = [T(A.f32(T1)) for _ in range(2)]
        mT = T(v3(A.bf16(8 * T1), 8))
        wfs = [T(v3(A.bf16(3072), 24)) for _ in range(2)]
        ost = [T(A.f32(640)) for _ in range(2)]
        assert A.off <= NW, A.off
        p1_end = A.off

        s.op("pool", lambda e: e.memset(va_ap[:, :, :, 64:65], 1.0), writes=[r["va"] for r in ring])
        s.op("pool", lambda e: e.memset(vb_ap[:, :, :, 64:65], 1.0), writes=[r["vb"] for r in ring])

        cnt = dict(xin=0, xres=0, pb=0, pa=0, otok=0, rden=0, wf=0, ost=0, pt=0, bank=0, xsb=0, qraw=0)

        def rr(name, n):
            v = cnt[name] % n
            cnt[name] += 1
            return v

        def transpose_to(dst_ap, src_T, src_cols, nq, evac_eng, dstT):
            pt = pT[rr("pt", 2)]
            ptv = v3(pt.ap, 8)
            for k in range(8):
                s.op("pe", lambda e, k=k, ptv=ptv: e.transpose(out=ptv[:, k, 0:nq], in_=src_cols(k), identity=ident.ap[0:nq, 0:nq]),
                     reads=[src_T, ident], writes=[pt])
            if evac_eng == "act":
                s.op("act", lambda e, ptv=ptv: e.activation(out=dst_ap, in_=ptv[:, :, 0:nq], func=AF.Copy), reads=[pt], writes=[dstT])
            else:
                s.op("dve", lambda e, ptv=ptv: e.tensor_copy(out=dst_ap, in_=ptv[:, :, 0:nq]), reads=[pt], writes=[dstT])

        def p1_load_x(p0, j):
            xi = xin[rr("xin", 2)]
            p0["xi"][j] = xi
            s.dma(f"xin{cnt['xin'] % 2}", lambda e: e.dma_start(out=xi.ap[0:p0["nq"], :], in_=p0["xsrc"](j)), writes=[xi])

        def p1_stage0a(p0, j):
            nq = p0["nq"]
            xi = p0["xi"][j]
            xb = xsb[rr("xsb", 2)]
            p0["xb"][j] = xb
            ssT, rsT = new_ss()
            s.op("act", lambda e: e.activation(out=xb.ap[0:nq, :], in_=xi.ap[0:nq, :], func=AF.Square, accum_out=ssT.ap[0:nq, 0:1]),
                 reads=[xi], writes=[ssT, xb])
            emit_rstd(ssT, rsT, nq, ncols=1)
            s.op("dve", lambda e: e.tensor_scalar(out=xb.ap[0:nq, :], in0=xi.ap[0:nq, :], scalar1=rsT.ap[0:nq], scalar2=None, op0=ALU.mult),
                 reads=[xi, rsT], writes=[xb])
            if j + 2 < p0["nsub"]:
                p1_load_x(p0, j + 2)

        def p1_stage0b(p0, j):
            nq = p0["nq"]
            xb = p0["xb"][j]
            transpose_to(hT.ap[:, :, j * nq:(j + 1) * nq], xb, lambda k: xb.ap[0:nq, k * 128:(k + 1) * 128], nq, "act", hT)

        def p1_tile(xsrc, x1dst, nsub, nq, rope_src, blocksA, blocksB, newA, newB, outs, pre_out=None, per_out=None, mid_out=None,
                    rope_load=True, rope_next=None):
            Tt = nsub * nq
            if rope_load:
                s.dma("rope", lambda e: e.dma_start(out=ropeT.ap[:, :, 0:Tt], in_=rope_src), writes=[ropeT])
            s.checkpoint()
            def proj_fm(col0, bank):
                for kc in range(8):
                    s.op("pe", lambda e, kc=kc: e.matmul(bank.ap[:, 0:Tt], lhsT=w1.ap[:, kc, col0:col0 + 128], rhs=hT.ap[:, kc, 0:Tt],
                                                         start=(kc == 0), stop=(kc == 7)), reads=[w1, hT], writes=[bank])

            def rope_finish(b1, b2, dst_list):
                s.op("dve", lambda e: e.tensor_tensor(out=rtmp[0].ap[:, 0:Tt], in0=b1.ap[:, 0:Tt], in1=ropeT.ap[:, 0, 0:Tt], op=ALU.mult),
                     reads=[b1, ropeT], writes=[rtmp[0]])
                s.op("dve", lambda e: e.tensor_tensor(out=rtmp[1].ap[:, 0:Tt], in0=b2.ap[:, 0:Tt], in1=ropeT.ap[:, 1, 0:Tt], op=ALU.mult),
                     reads=[b2, ropeT], writes=[rtmp[1]])
                for dst_ap, c0, n, bufs in dst_list:
                    s.op("dve", lambda e, dst_ap=dst_ap, c0=c0, n=n: e.tensor_tensor(out=dst_ap, in0=rtmp[0].ap[:, c0:c0 + n], in1=rtmp[1].ap[:, c0:c0 + n],
                                                                                  op=ALU.add), reads=[rtmp[0], rtmp[1]], writes=bufs)

            def raw_tile(col0):
                b1 = pbank[rr("bank", 6)]
                proj_fm(col0, b1)
                qr = qraw[rr("qraw", 2)]
                s.op("act", lambda e: e.activation(out=qr.ap[:, 0:Tt], in_=b1.ap[:, 0:Tt], func=AF.Copy), reads=[b1], writes=[qr])
                return b1, qr

            def perm_mm(pi, qr):
                b2 = pbank[rr("bank", 6)]
                s.op("pe", lambda e: e.matmul(b2.ap[:, 0:Tt], lhsT=perms.ap[:, pi, :], rhs=qr.ap[:, 0:Tt], start=True, stop=True),
                     reads=[perms, qr], writes=[b2])
                return b2

            prev = None
            for i in range(5):
                cur = raw_tile(QA + i * 128) if i < 4 else None
                if prev is not None:
                    pi_, (pb1, pqr) = prev
                    b2 = perm_mm(0, pqr)
                    rope_finish(pb1, b2, [(qaT.ap[:, pi_, 0:Tt], 0, Tt, [qaT])])
                prev = (i, cur) if cur is not None else None
            s.sub('qa')
            kb1, kqr = raw_tile(KA)
            for g in range(2):
                bd = perm_mm(1 + g, kqr)
                br = perm_mm(3 + g, kqr)
                rope_finish(bd, br, [(kaT_ap[:, g, newA(j) * 128:newA(j) * 128 + nq], j * nq, nq, [ring[newA(j)]["ka"]]) for j in range(nsub)])
            s.sub('ka')
            for i in range(4):
                b1 = pbank[rr("bank", 6)]
                proj_fm(QB + i * 128, b1)
                s.op("act", lambda e, i=i, b1=b1: e.activation(out=qbT.ap[:, i, 0:Tt], in_=b1.ap[:, 0:Tt], func=AF.Copy),
                     reads=[b1], writes=[qbT])
            for i in range(4):
                b1 = pbank[rr("bank", 6)]
                proj_fm(KB + i * 128, b1)
                for j in range(nsub):
                    rb = newB(j)
                    s.op("act", lambda e, i=i, j=j, rb=rb, b1=b1: e.activation(out=kbT_ap[:, i, rb * 128:rb * 128 + nq],
                                                                               in_=b1.ap[:, j * nq:(j + 1) * nq], func=AF.Copy),
                         reads=[b1], writes=[ring[rb]["kb"]])
            s.sub('qkb')
            for j in range(nsub):
                pp = ppairs[rr("bank", 6) % 3]
                for (c0, n, o0) in ((VB, 512, 0), (VA, 128, 512)):
                    for kc in range(8):
                        s.op("pe", lambda e, kc=kc, c0=c0, n=n, o0=o0, pp=pp, j=j: e.matmul(
                            pp.ap[0:nq, o0:o0 + n], lhsT=hT.ap[:, kc, j * nq:(j + 1) * nq], rhs=w1.ap[:, kc, c0:c0 + n],
                            start=(kc == 0), stop=(kc == 7)), reads=[w1, hT], writes=[pp])
                rbA, rbB = newA(j), newB(j)
                s.op("dve", lambda e, pp=pp, rbB=rbB: e.tensor_copy(out=vb_ap[0:nq, rbB, :, 0:64], in_=v3(pp.ap[0:nq, 0:512], 8)),
                     reads=[pp], writes=[ring[rbB]["vb"]])
                s.op("dve", lambda e, pp=pp, rbA=rbA: e.tensor_copy(out=va_ap[0:nq, rbA, :, 0:64], in_=v3(pp.ap[0:nq, 512:640], 2)),
                     reads=[pp], writes=[ring[rbA]["va"]])
                s.sub('v')
                if outs is not None:
                    so = ost[rr("ost", 2)]
                    import os as _os
                    if _os.environ.get("KVAR", "") == "dve":
                        s.op("dve", lambda e, pp=pp, so=so: e.tensor_copy(out=so.ap[0:nq, 0:512], in_=pp.ap[0:nq, 0:512]), reads=[pp], writes=[so])
                        s.op("dve", lambda e, pp=pp, so=so: e.tensor_copy(out=so.ap[0:nq, 512:640], in_=pp.ap[0:nq, 512:640]), reads=[pp], writes=[so])
                    elif _os.environ.get("KVAR", "") == "low":
                        s.op("act", lambda e, pp=pp, so=so: e.activation(out=junkT.ap.bitcast(F32)[0:nq, 0:512], in_=pp.ap[0:nq, 0:512], func=AF.Copy),
                             reads=[pp], writes=[junkT])
                    else:
                        kv = _os.environ.get("KVAR", "")
                        fn_ = AF.Identity if kv == "ident" else AF.Copy
                        if kv != "second":
                            s.op("act", lambda e, pp=pp, so=so: e.activation(out=so.ap[0:nq, 0:512], in_=pp.ap[0:nq, 0:512], func=fn_),
                                 reads=[pp], writes=[so])
                        if kv != "first":
                            s.op("act", lambda e, pp=pp, so=so: e.activation(out=so.ap[0:nq, 512:640], in_=pp.ap[0:nq, 512:640], func=fn_),
                                 reads=[pp], writes=[so])
                    s.sub('vcopy')
                    dbv = outs["bv"](j)
                    if dbv is not None:
                        s.dma(mchan(), lambda e, so=so, dbv=dbv: e.dma_start(out=dbv, in_=so.ap[0:nq, 0:512]), reads=[so])
                    s.sub('vdma1')
                    dav = outs["av"](j)
                    if dav is not None:
                        s.dma(mchan(), lambda e, so=so, dav=dav: e.dma_start(out=dav, in_=so.ap[0:nq, 512:640]), reads=[so])
                    s.sub('vout')
                    dbk = outs["bk"](j)
                    if dbk is not None:
                        b1 = pbank[rr("bank", 6)]
                        for kc in range(8):
                            s.op("pe", lambda e, kc=kc, b1=b1, j=j: e.matmul(b1.ap[0:nq, :], lhsT=hT.ap[:, kc, j * nq:(j + 1) * nq],
                                                                             rhs=w1.ap[:, kc, KB:KB + 512], start=(kc == 0), stop=(kc == 7)),
                                 reads=[w1, hT], writes=[b1])
                        so2 = ost[rr("ost", 2)]
                        s.op("act", lambda e, b1=b1, so2=so2: e.activation(out=so2.ap[0:nq, 0:512], in_=b1.ap[0:nq, :], func=AF.Copy),
                             reads=[b1], writes=[so2])
                        s.dma(mchan(), lambda e, so2=so2, dbk=dbk: e.dma_start(out=dbk, in_=so2.ap[0:nq, 0:512]), reads=[so2])
                    s.sub('bkout')
                    dak = outs["ak"](j)
                    if dak is not None:
                        b1 = pbank[rr("bank", 6)]
                        for kc in range(8):
                            s.op("pe", lambda e, kc=kc, b1=b1, j=j: e.matmul(b1.ap[0:nq, 0:128], lhsT=hT.ap[:, kc, j * nq:(j + 1) * nq],
                                                                             rhs=w1.ap[:, kc, KA:KA + 128], start=(kc == 0), stop=(kc == 7)),
                                 reads=[w1, hT], writes=[b1])
                        rtm = outs["ropetm"]
                        so3 = ost[rr("ost", 2)]
                        kv = v4(b1.ap[0:nq, 0:128], 2, 2)
                        t1 = v4(so3.ap[0:nq, 0:128], 2, 2)
                        t2 = v4(so3.ap[0:nq, 128:256], 2, 2)
                        o3 = v4(so3.ap[0:nq, 256:384], 2, 2)
                        cosb = rtm.ap[0:nq, 0:32].unsqueeze(1).unsqueeze(1).to_broadcast([nq, 2, 2, 32])
                        sinb = rtm.ap[0:nq, 32:64].unsqueeze(1).to_broadcast([nq, 2, 32])
                        s.op("dve", lambda e, kv=kv, t1=t1, cosb=cosb: e.tensor_tensor(out=t1, in0=kv, in1=cosb, op=ALU.mult), reads=[b1, rtm], writes=[so3])
                        for hf in range(2):
                            s.op("dve", lambda e, kv=kv, t2=t2, sinb=sinb, hf=hf: e.tensor_tensor(out=t2[:, :, hf, :], in0=kv[:, :, 1 - hf, :], in1=sinb, op=ALU.mult),
                                 reads=[b1, rtm], writes=[so3])
                        s.op("dve", lambda e, t1=t1, t2=t2, o3=o3: e.tensor_tensor(out=o3[:, :, 0, :], in0=t1[:, :, 0, :], in1=t2[:, :, 0, :], op=ALU.subtract),
                             reads=[so3], writes=[so3])
                        s.op("dve", lambda e, t1=t1, t2=t2, o3=o3: e.tensor_tensor(out=o3[:, :, 1, :], in0=t1[:, :, 1, :], in1=t2[:, :, 1, :], op=ALU.add),
                             reads=[so3], writes=[so3])
                        s.dma(mchan(), lambda e, so3=so3, dak=dak: e.dma_start(out=dak, in_=so3.ap[0:nq, 256:384]), reads=[so3])

            s.checkpoint()
            xr_of = {}

            def load_xres(j):
                xr = xres[rr("xres", 2)]
                xr_of[j] = xr
                s.dma(f"xres{cnt['xres'] % 2}", lambda e, xr=xr, j=j: e.dma_start(out=xr.ap[0:nq, :], in_=xsrc(j)), writes=[xr])

            wf_of = {}

            def load_wf(f):
                w = wfs[rr("wf", 2)]
                wf_of[f] = w
                s.dma(f"wf{cnt['wf'] % 2}", lambda e, w=w, f=f: e.dma_start(out=w.ap, in_=v3(x_wf[f, :, :], 24)), reads=[b_wf], writes=[w])

            if rope_next is not None:
                nsrc, nT = rope_next
                s.dma("rope", lambda e: e.dma_start(out=ropeT.ap[:, :, 0:nT], in_=nsrc), writes=[ropeT])
            for j in range(min(2, nsub)):
                load_xres(j)
            load_wf(0)
            load_wf(1)

            Opair = ppairs[2]
            Ov = Opair.ap.rearrange("p (g c) -> p g c", g=2)[:, :, 0:260].rearrange("p g (h d) -> p g h d", d=65)

            def attn_units(j):
                units = []
                blksB = blocksB(j)
                blksA = blocksA(j)
                for h in range(8):
                    units.append(("B", h, blksB))
                for hp in range(4):
                    units.append(("A", hp, blksA))
                return units

            slot_pairs = [ppairs[0], ppairs[1], ppairs[3]]
            LOOK = 2

            def emit_scores(u, j, slot):
                kind, hx, blks = u
                pp = slot_pairs[slot]
                if kind == "B":
                    ti, p0 = hx // 2, 64 * (hx % 2)
                    for bi, (rb, eb, nk) in enumerate(blks):
                        s.op("pe", lambda e, bi=bi, rb=rb, nk=nk, ti=ti, p0=p0, pp=pp: e.matmul(
                            pp.ap[0:nk, bi * 128:bi * 128 + nq], lhsT=kbT_ap[p0:p0 + 64, ti, rb * 128:rb * 128 + nk],
                            rhs=qbT.ap[p0:p0 + 64, ti, j * nq:(j + 1) * nq], start=True, stop=True),
                            reads=[ring[rb]["kb"], qbT], writes=[pp])
                else:
                    for hh in range(2):
                        h = 2 * hx + hh
                        g = h // 4
                        ti, p0 = h // 2, 64 * (h % 2)
                        for bi, (rb, eb, nk) in enumerate(blks):
                            s.op("pe", lambda e, bi=bi, rb=rb, nk=nk, ti=ti, p0=p0, pp=pp, hh=hh, g=g: e.matmul(
                                pp.ap[0:nk, hh * 512 + bi * 128:hh * 512 + bi * 128 + nq],
                                lhsT=kaT_ap[p0:p0 + 64, g, rb * 128:rb * 128 + nk],
                                rhs=qaT.ap[p0:p0 + 64, ti, j * nq:(j + 1) * nq], start=True, stop=True),
                                reads=[ring[rb]["ka"], qaT], writes=[pp])

            def emit_softmax_pv(u, j, slot):
                kind, hx, blks = u
                pp = slot_pairs[slot]
                nb = len(blks)
                nfull = sum(1 for b in blks if b[2] == 128)
                eb0 = blks[0][1]
                if kind == "B":
                    P = pB[rr("pb", 3)]
                    if nfull:
                        n0 = min(nfull, 4)
                        s.op("act", lambda e, P=P, pp=pp: e.activation(out=P.ap[:, 0:n0, 0:nq], in_=v3(pp.ap[:, 0:512], 4)[:, 0:n0, 0:nq],
                                                                       func=AF.Exp, scale=0.125), reads=[pp], writes=[P])
                        if nfull == 5:
                            s.op("act", lambda e, P=P, pp=pp: e.activation(out=P.ap[:, 4, 0:nq], in_=pp.ap[:, 512:512 + nq],
                                                                           func=AF.Exp, scale=0.125), reads=[pp], writes=[P])
                        s.op("dve", lambda e, P=P: e.tensor_tensor(out=P.ap[:, 0:nfull, 0:nq], in0=P.ap[:, 0:nfull, 0:nq],
                                                                    in1=e_b.ap[:, hx, eb0:eb0 + nfull, 0:nq], op=ALU.mult),
                             reads=[P, e_b], writes=[P])
                    for bi in range(nfull, nb):
                        rb, eb, nk = blks[bi]
                        s.op("act", lambda e, P=P, pp=pp, bi=bi, nk=nk: e.activation(out=P.ap[0:nk, bi, 0:nq], in_=pp.ap[0:nk, bi * 128:bi * 128 + nq],
                                                                                     func=AF.Exp, scale=0.125), reads=[pp], writes=[P])
                        s.op("dve", lambda e, P=P, bi=bi, nk=nk, eb=eb: e.tensor_tensor(out=P.ap[0:nk, bi, 0:nq], in0=P.ap[0:nk, bi, 0:nq],
                                                                                        in1=e_b.ap[0:nk, hx, eb, 0:nq], op=ALU.mult),
                             reads=[P, e_b], writes=[P])
                    h = hx
                    for bi, (rb, eb, nk) in enumerate(blks):
                        s.op("pe", lambda e, P=P, bi=bi, rb=rb, nk=nk, h=h: e.matmul(
                            Ov[0:nq, h // 4, h % 4, :], lhsT=P.ap[0:nk, bi, 0:nq], rhs=vb_ap[0:nk, rb, h, 0:65],
                            start=(bi == 0), stop=(bi == nb - 1)), reads=[P, ring[rb]["vb"]], writes=[Opair])
                else:
                    P = pA[rr("pa", 2)]
                    ppv = pp.ap.rearrange("p (h c) -> p h c", h=2)[:, :, 0:256].rearrange("p h (b q) -> p h b q", b=2)
                    if nfull:
                        for hh in range(2):
                            s.op("act", lambda e, P=P, ppv=ppv, hh=hh: e.activation(out=P.ap[:, hh, 0:nfull, 0:nq], in_=ppv[:, hh, 0:nfull, 0:nq],
                                                                                   func=AF.Exp, scale=0.125), reads=[pp], writes=[P])
                        s.op("dve", lambda e, P=P: e.tensor_tensor(out=P.ap[:, :, 0:nfull, 0:nq], in0=P.ap[:, :, 0:nfull, 0:nq],
                                                                    in1=e_a.ap[:, :, eb0:eb0 + nfull, 0:nq], op=ALU.mult),
                             reads=[P, e_a], writes=[P])
                    for bi in range(nfull, nb):
                        rb, eb, nk = blks[bi]
                        for hh in range(2):
                            s.op("act", lambda e, P=P, ppv=ppv, bi=bi, nk=nk, hh=hh: e.activation(out=P.ap[0:nk, hh, bi, 0:nq], in_=ppv[0:nk, hh, bi, 0:nq],
                                                                                                  func=AF.Exp, scale=0.125), reads=[pp], writes=[P])
                        s.op("dve", lambda e, P=P, bi=bi, nk=nk, eb=eb: e.tensor_tensor(out=P.ap[0:nk, :, bi, 0:nq], in0=P.ap[0:nk, :, bi, 0:nq],
                                                                                        in1=e_a.ap[0:nk, :, eb, 0:nq], op=ALU.mult),
                             reads=[P, e_a], writes=[P])
                    for hh in range(2):
                        h = 2 * hx + hh
                        g = h // 4
                        for bi, (rb, eb, nk) in enumerate(blks):
                            s.op("pe", lambda e, P=P, bi=bi, rb=rb, nk=nk, h=h, hh=hh, g=g: e.matmul(
                                Ov[0:nq, h // 4, h % 4, :], lhsT=P.ap[0:nk, hh, bi, 0:nq], rhs=va_ap[0:nk, rb, g, 0:65],
                                start=(bi == 0), stop=(bi == nb - 1)), reads=[P, ring[rb]["va"]], writes=[Opair])

            def emit_norm(kind, j, ot):
                rd = rden[rr("rden", 4)]
                c0 = 0 if kind == "A" else 512
                for g in range(2):
                    rdg = rd.ap[0:nq, g * 4:(g + 1) * 4]
                    den = Ov[0:nq, g, :, 64]
                    if kind == "A":
                        s.op("dve", lambda e, rdg=rdg, den=den, g=g: e.tensor_tensor(out=rdg, in0=den, in1=esink.ap[0:nq, g * 4:(g + 1) * 4], op=ALU.add),
                             reads=[Opair, esink], writes=[rd])
                        s.op("dve", lambda e, rdg=rdg: e.reciprocal(out=rdg, in_=rdg), reads=[rd], writes=[rd])
                    else:
                        s.op("dve", lambda e, rdg=rdg, den=den: e.reciprocal(out=rdg, in_=den), reads=[Opair], writes=[rd])
                    s.op("dve", lambda e, rdg=rdg, g=g: e.tensor_tensor(out=v3(ot.ap[0:nq, c0 + g * 256:c0 + (g + 1) * 256], 4), in0=Ov[0:nq, g, :, 0:64],
                                                                  in1=rdg.unsqueeze(2).to_broadcast([nq, 4, 64]), op=ALU.mult),
                         reads=[Opair, rd], writes=[ot])

            ot = otok[0]
            allunits = [(j, u) for j in range(nsub) for u in attn_units(j)]
            nun = len(allunits)

            def emit_tr(j):
                transpose_to(oT.ap[:, :, j * nq:(j + 1) * nq], ot, lambda k: ot.ap[0:nq, k * 128:(k + 1) * 128], nq, "dve", oT)

            pending_tr = None
            for k in range(min(LOOK, nun)):
                emit_scores(allunits[k][1], allunits[k][0], k % 3)
            for k, (j, u) in enumerate(allunits):
                if k + LOOK < nun:
                    emit_scores(allunits[k + LOOK][1], allunits[k + LOOK][0], (k + LOOK) % 3)
                emit_softmax_pv(u, j, k % 3)
                ui = k % 12
                if ui == 2 and pending_tr is not None:
                    emit_tr(pending_tr)
                    pending_tr = None
                if ui == 7:
                    emit_norm("B", j, ot)
                if ui == 11:
                    emit_norm("A", j, ot)
                    pending_tr = j
            if pending_tr is not None:
                emit_tr(pending_tr)

            s.checkpoint()
            for f in range(8):
                w = wf_of[f]
                bk = [pbank[4 * (f % 2) + i] for i in range(4)]
                for kc in range(4):
                    s.op("pe", lambda e, kc=kc, w=w, bk=bk: e.matmul(bk[0].ap[:, 0:Tt], lhsT=w.ap[:, kc, :], rhs=oT.ap[:, kc, 0:Tt],
                                                                     start=(kc == 0), stop=(kc == 3)), reads=[w, oT], writes=[bk[0]])
                for kc in range(4):
                    s.op("pe", lambda e, kc=kc, w=w, bk=bk: e.matmul(bk[1].ap[:, 0:Tt], lhsT=w.ap[:, 4 + kc, :], rhs=oT.ap[:, 4 + kc, 0:Tt],
                                                                     start=(kc == 0), stop=(kc == 3)), reads=[w, oT], writes=[bk[1]])
                for gi in range(2):
                    for kc in range(8):
                        s.op("pe", lambda e, kc=kc, w=w, bk=bk, gi=gi: e.matmul(bk[2 + gi].ap[:, 0:Tt], lhsT=w.ap[:, 8 + 8 * gi + kc, :],
                                                                                rhs=hT.ap[:, kc, 0:Tt], start=(kc == 0), stop=(kc == 7)),
                             reads=[w, hT], writes=[bk[2 + gi]])
                if f + 2 < 8:
                    load_wf(f + 2)
                for gi in range(2):
                    s.op("act", lambda e, gi=gi, bk=bk: e.activation(out=tab[gi].ap[:, 0:Tt], in_=bk[2 + gi].ap[:, 0:Tt], func=AF.Tanh, scale=0.5),
                         reads=[bk[2 + gi]], writes=[tab[gi]])
                    s.op("dve", lambda e, gi=gi, bk=bk: e.scalar_tensor_tensor(out=t12[gi].ap[:, 0:Tt], in0=tab[gi].ap[:, 0:Tt], scalar=1.0,
                                                                               in1=bk[gi].ap[:, 0:Tt], op0=ALU.add, op1=ALU.mult),
                         reads=[tab[gi], bk[gi]], writes=[t12[gi]])
                s.op("dve", lambda e, f=f: e.tensor_tensor(out=mT.ap[:, f, 0:Tt], in0=t12[0].ap[:, 0:Tt], in1=t12[1].ap[:, 0:Tt], op=ALU.add),
                     reads=[t12[0], t12[1]], writes=[mT])
            if pre_out is not None:
                pre_out()
            for j in range(nsub):
                zp = ppairs[j % 2]
                for hf in range(2):
                    for kc in range(8):
                        s.op("pe", lambda e, kc=kc, hf=hf, zp=zp, j=j: e.matmul(zp.ap[0:nq, hf * 512:(hf + 1) * 512],
                                                                                lhsT=mT.ap[:, kc, j * nq:(j + 1) * nq],
                                                                                rhs=wout.ap[:, kc, hf * 512:(hf + 1) * 512],
                                                                                start=(kc == 0), stop=(kc == 7)), reads=[mT, wout], writes=[zp])
                if mid_out is not None:
                    mid_out(j)
                ssT, rsT = new_ss()
                for hf in range(2):
                    s.op("act", lambda e, zp=zp, ssT=ssT, hf=hf: e.activation(out=junk[0:nq, hf * 512:(hf + 1) * 512], in_=zp.ap[0:nq, hf * 512:(hf + 1) * 512],
                                                                          func=AF.Square, accum_out=ssT.ap[0:nq, hf:hf + 1]), reads=[zp], writes=[ssT, junkT])
                emit_rstd(ssT, rsT, nq)
                xr = xr_of[j]
                for hf in range(2):
                    s.op("dve", lambda e, zp=zp, rsT=rsT, hf=hf: e.scalar_tensor_tensor(out=tmpz.ap[0:nq, hf * 512:(hf + 1) * 512], in0=zp.ap[0:nq, hf * 512:(hf + 1) * 512],
                                                                                   scalar=rsT.ap[0:nq], in1=gpost.ap[0:nq, hf * 512:(hf + 1) * 512],
                                                                                   op0=ALU.mult, op1=ALU.mult),
                         reads=[zp, rsT, gpost], writes=[tmpz])
                s.op("dve", lambda e, xr=xr: e.tensor_tensor(out=xr.ap[0:nq, :], in0=tmpz.ap[0:nq, :], in1=xr.ap[0:nq, :], op=ALU.add),
                     reads=[tmpz, xr], writes=[xr])
                s.dma(mchan(), lambda e, xr=xr, j=j: e.dma_start(out=x1dst(j), in_=xr.ap[0:nq, :]), reads=[xr], writes=[b_x1])
                if j + 2 < nsub:
                    load_xres(j + 2)
                if per_out is not None:
                    per_out(j)

        tiles1 = []
        if sample:
            cs = cstg
            cb = cstb
            s.dma(mchan(), lambda e: e.dma_start(out=v3(cs.ap, 4), in_=d_cbk.rearrange("(b p) c -> p b c", p=128)), writes=[cs])
            s.op("dve", lambda e: e.tensor_copy(out=cb.ap, in_=cs.ap), reads=[cs], writes=[cb])
            for blk in range(4):
                pt = pT[rr("pt", 2)]
                ptv = v3(pt.ap, 8)
                for ti in range(4):
                    s.op("pe", lambda e, blk=blk, ti=ti, ptv=ptv: e.transpose(out=ptv[:, ti, :], in_=cb.ap[:, blk * 512 + ti * 128: blk * 512 + (ti + 1) * 128],
                                                                            identity=ident.ap), reads=[cb, ident], writes=[pt])
                s.op("dve", lambda e, blk=blk, ptv=ptv: e.tensor_copy(out=kbT_ap[:, :, blk * 128:(blk + 1) * 128], in_=ptv[:, 0:4, :]),
                     reads=[pt], writes=[ring[blk]["kb"]])
            s.dma(mchan(), lambda e: e.dma_start(out=v3(cs.ap, 4), in_=d_cbv.rearrange("(b p) c -> p b c", p=128)), writes=[cs])
            s.op("dve", lambda e: e.tensor_copy(out=vb_ap[:, 0:4, :, 0:64], in_=v4(cs.ap, 4, 8)), reads=[cs],
                 writes=[ring[b]["vb"] for b in range(4)])
            s.dma(mchan(), lambda e: e.dma_start(out=cs.ap[:, 0:128], in_=d_cak[:, :]), writes=[cs])
            s.dma(mchan(), lambda e: e.dma_start(out=cs.ap[:, 128:256], in_=d_cav[:, :]), writes=[cs])
            s.op("dve", lambda e: e.tensor_copy(out=v4(cb.ap[:, 0:256], 2, 2), in_=v3(cs.ap[:, 0:128], 2).unsqueeze(2).to_broadcast([128, 2, 2, 64])),
                 reads=[cs], writes=[cb])
            pt = pT[rr("pt", 2)]
            ptv = v3(pt.ap, 8)
            for ti in range(2):
                s.op("pe", lambda e, ti=ti, ptv=ptv: e.transpose(out=ptv[:, ti, :], in_=cb.ap[:, ti * 128:(ti + 1) * 128], identity=ident.ap),
                     reads=[cb, ident], writes=[pt])
            s.op("dve", lambda e, ptv=ptv: e.tensor_copy(out=kaT_ap[:, :, 0:128], in_=ptv[:, 0:2, :]), reads=[pt], writes=[ring[0]["ka"]])
            s.op("dve", lambda e: e.tensor_copy(out=va_ap[:, 0, :, 0:64], in_=v3(cs.ap[:, 128:256], 2)), reads=[cs], writes=[ring[0]["va"]])
            s.checkpoint()
            rtm_s = T(A.f32(64))
            s.dma(mchan(), lambda e: e.dma_start(out=rtm_s.ap[0:16, :], in_=d_ropetm_s[:, :]), writes=[rtm_s])
            outs_s = dict(bv=lambda j: o_bvs[:, :], av=lambda j: o_avs[:, :], bk=lambda j: o_bks[:, :], ak=lambda j: o_aks[:, :], ropetm=rtm_s)
            tiles1.append(dict(xsrc=lambda j: d_xs[:, :], x1dst=lambda j: x_x1s[:, :], nsub=1, nq=16, rope_src=d_ropefm_s[:, :, :],
                               blocksA=lambda j: [(0, 0, 128), (1, 1, 16)],
                               blocksB=lambda j: [(0, 0, 128), (1, 1, 128), (2, 2, 128), (3, 3, 128), (4, 4, 16)],
                               newA=lambda j: 1, newB=lambda j: 4, outs=outs_s))
        else:
            rtm_s = T(A.f32(64))
        rtm_p = T(A.f32(64))
        s.dma(mchan(), lambda e: e.dma_start(out=rtm_p.ap, in_=d_ropetm[:, :]), writes=[rtm_p])

        s.checkpoint()
        ntile = S // T1
        for sq in range(NSEQ):
            for t in range(ntile):
                last = (t == ntile - 1)
                qb0 = 4 * t

                def blocksB(j, qb0=qb0):
                    qb = qb0 + j
                    return [(kb % 8, 4 - (qb - kb), 128) for kb in range(max(0, qb - 4), qb + 1)]

                def blocksA(j, qb0=qb0):
                    qb = qb0 + j
                    return [(kb % 8, 1 - (qb - kb), 128) for kb in range(max(0, qb - 1), qb + 1)]

                def newblk(j, qb0=qb0):
                    return (qb0 + j) % 8

                outs = None
                if last:
                    def rows_b(j, t=t):
                        return (t * T1 + j * 128) - (S - KB_ROWS)
                    outs = dict(
                        bv=lambda j, sq=sq, rows_b=rows_b: o_bvp[sq, rows_b(j):rows_b(j) + 128, :] if rows_b(j) >= 0 else None,
                        bk=lambda j, sq=sq, rows_b=rows_b: o_bkp[sq, rows_b(j):rows_b(j) + 128, :] if rows_b(j) >= 0 else None,
                        av=lambda j, sq=sq: o_avp[sq, :, :] if j == 3 else None,
                        ak=lambda j, sq=sq: o_akp[sq, :, :] if j == 3 else None,
                        ropetm=rtm_p)
                tiles1.append(dict(xsrc=lambda j, sq=sq, t=t: d_xp[sq, t * T1 + j * 128: t * T1 + (j + 1) * 128, :],
                                   x1dst=lambda j, sq=sq, t=t: x_x1p[sq, t * T1 + j * 128: t * T1 + (j + 1) * 128, :],
                                   nsub=4, nq=128, rope_src=d_ropefm[:, :, t * T1:(t + 1) * T1],
                                   blocksA=blocksA, blocksB=blocksB, newA=newblk, newB=newblk, outs=outs))

        p0s = [dict(xsrc=tk["xsrc"], nsub=tk["nsub"], nq=tk["nq"], xi={}, xb={}) for tk in tiles1]
        for j in range(min(2, p0s[0]["nsub"])):
            p1_load_x(p0s[0], j)
        for j in range(p0s[0]["nsub"]):
            p1_stage0a(p0s[0], j)
            p1_stage0b(p0s[0], j)
        for i, tk in enumerate(tiles1):
            nx = p0s[i + 1] if i + 1 < len(tiles1) else None
            if nx is not None:
                for j in range(min(2, nx["nsub"])):
                    p1_load_x(nx, j)

            def pre_out(nx=nx):
                if nx is not None:
                    p1_stage0a(nx, 0)

            def mid_out(j, nx=nx):
                if nx is not None and j + 1 < nx["nsub"]:
                    p1_stage0a(nx, j + 1)

            def per_out(j, nx=nx):
                if nx is not None and j < nx["nsub"]:
                    p1_stage0b(nx, j)

            nk = tiles1[i + 1] if i + 1 < len(tiles1) else None
            p1_tile(pre_out=pre_out, per_out=per_out, mid_out=mid_out, rope_load=(i == 0),
                    rope_next=((nk["rope_src"], nk["nsub"] * nk["nq"]) if nk is not None else None), **tk)
            if nx is not None:
                for j in range(tk["nsub"], nx["nsub"]):
                    p1_stage0b(nx, j)
                    if j + 1 < nx["nsub"]:
                        p1_stage0a(nx, j + 1)

        s.checkpoint()
        s.barrier()
        A.off = base_off
        wup = T(v3(A.bf16(8 * 2 * DFF), 8))
        wdn = T(v3(A.bf16(NFT * D), NFT))
        g2post = T(A.f32(D))
        g2pre = T(A.f32(8))
        cw = T(v3(A.f32(NFT * 3), NFT))
        cbias = T(A.f32(NFT))
        carry = T(v3(A.f32(NFT * 2), NFT))
        nodep.extend([wup, wdn])
        act2_off = A.off
        st2 = [T(A.f32(2816)) for _ in range(4)]
        s.dma(mchan(), lambda e: e.dma_start(out=g2pre.ap, in_=d_g2pre[:, :]), writes=[g2pre])
        s.dma(mchan(), lambda e: e.dma_start(out=g2post.ap, in_=d_g2post[0:1, :].partition_broadcast(128)), writes=[g2post])
        s.dma(mchan(), lambda e: e.dma_start(out=cw.ap, in_=v3(d_cw, NFT)), writes=[cw])
        s.dma(mchan(), lambda e: e.dma_start(out=cbias.ap, in_=d_cb[:, :]), writes=[cbias])
        i2 = 0
        for kc in range(8):
            for hf in range(2):
                stg = st2[i2 % 4]
                s.dma(f"st{i2 % 4}", lambda e, stg=stg, kc=kc, hf=hf: e.dma_start(out=stg.ap, in_=d_wup[kc * 128:(kc + 1) * 128, hf * DFF:(hf + 1) * DFF]),
                      writes=[stg])
                cvt(wup.ap[:, kc, hf * DFF:(hf + 1) * DFF], stg.ap, g2pre.ap[:, kc:kc + 1], [stg, g2pre], [wup])
                i2 += 1
        for kc in range(NFT):
            stg = st2[i2 % 4]
            s.dma(f"st{i2 % 4}", lambda e, stg=stg, kc=kc: e.dma_start(out=stg.ap[:, 0:D], in_=d_wdn[kc * 128:(kc + 1) * 128, :]), writes=[stg])
            cvt(wdn.ap[:, kc, :], stg.ap[:, 0:D], 1.0, [stg], [wdn])
            i2 += 1
        s.barrier()
        A.off = act2_off
        xin2 = [T(A.f32(D)) for _ in range(2)]
        xres2 = [T(A.f32(D)) for _ in range(2)]
        xsb2 = [T(A.bf16(D)) for _ in range(2)]
        h2T = T(v3(A.bf16(8 * T2), 8))
        gT = T(v3(A.bf16(NFT * T2), NFT))
        ubuf = [T(A.f32(T2 + 16)) for _ in range(2)]
        acc_off = A.off
        acc = [T(A.f32(T2)) for _ in range(2)]
        ge = [T(A.f32(T2)) for _ in range(2)]
        tmpz2 = T(arena_t[:, acc_off:acc_off + D], acc[0].bufs + acc[1].bufs)
        cvst = T(A.f32(512))
        assert A.off <= NW, A.off
        cnt2 = dict(xin=0, xres=0, ub=0, acc=0, ge=0, pt=0, zp=0, cv=0, xsb=0)

        def rr2(name, n):
            v = cnt2[name] % n
            cnt2[name] += 1
            return v

        class Tile2:
            def __init__(self, xsrc, ydst, nsub, nq, conv_dst, first):
                self.xsrc, self.ydst, self.nsub, self.nq, self.conv_dst, self.first = xsrc, ydst, nsub, nq, conv_dst, first
                self.Tt = nsub * nq
                self.xr_of = {}
                self.xi_of = {}
                self.xb_of = {}

        def p2_load_x(tl, j):
            xi = xin2[rr2("xin", 2)]
            tl.xi_of[j] = xi
            s.dma(f"xin{cnt2['xin'] % 2}", lambda e: e.dma_start(out=xi.ap[0:tl.nq, :], in_=tl.xsrc(j)), reads=[b_x1], writes=[xi])

        def p2_stage0a(tl, j):
            nq = tl.nq
            xi = tl.xi_of[j]
            xb = xsb2[rr2("xsb", 2)]
            tl.xb_of[j] = xb
            ssT, rsT = new_ss()
            jk = xb.ap
            s.op("act", lambda e: e.activation(out=jk[0:nq, :], in_=xi.ap[0:nq, :], func=AF.Square, accum_out=ssT.ap[0:nq, 0:1]),
                 reads=[xi], writes=[ssT, xb])
            emit_rstd(ssT, rsT, nq, ncols=1)
            s.op("dve", lambda e: e.tensor_scalar(out=xb.ap[0:nq, :], in0=xi.ap[0:nq, :], scalar1=rsT.ap[0:nq], scalar2=None, op0=ALU.mult),
                 reads=[xi, rsT], writes=[xb])
            if j + 2 < tl.nsub:
                p2_load_x(tl, j + 2)

        def p2_stage0b(tl, j):
            nq = tl.nq
            xb = tl.xb_of[j]
            pt = pT[rr2("pt", 2)]
            ptv = v3(pt.ap, 8)
            for k in range(8):
                s.op("pe", lambda e, k=k: e.transpose(out=ptv[:, k, 0:nq], in_=xb.ap[0:nq, k * 128:(k + 1) * 128], identity=ident.ap[0:nq, 0:nq]),
                     reads=[xb, ident], writes=[pt])
            s.op("act", lambda e: e.activation(out=h2T.ap[:, :, j * nq:(j + 1) * nq], in_=ptv[:, :, 0:nq], func=AF.Copy), reads=[pt], writes=[h2T])

        def p2_load_res(tl, j):
            xr = xres2[rr2("xres", 2)]
            tl.xr_of[j] = xr
            s.dma(f"xres{cnt2['xres'] % 2}", lambda e: e.dma_start(out=xr.ap[0:tl.nq, :], in_=tl.xsrc(j)), reads=[b_x1], writes=[xr])

        def p2_ffn(tl):
            Tt = tl.Tt
            if tl.first:
                if tl.first == "zero":
                    s.op("pool", lambda e: e.memset(carry.ap, 0.0), writes=[carry])
                else:
                    s.dma(mchan(), lambda e: e.dma_start(out=carry.ap, in_=v3(d_sconv, NFT)), writes=[carry])
            for j in range(min(2, tl.nsub)):
                p2_load_res(tl, j)
            pend_tail = [None]
            for f in range(NFT):
                bu = pbank[2 * (f % 4)]
                bv = pbank[2 * (f % 4) + 1]
                for kc in range(8):
                    s.op("pe", lambda e, kc=kc, f=f, bu=bu: e.matmul(bu.ap[:, 0:Tt], lhsT=wup.ap[:, kc, f * 128:(f + 1) * 128], rhs=h2T.ap[:, kc, 0:Tt],
                                                                     start=(kc == 0), stop=(kc == 7)), reads=[wup, h2T], writes=[bu])
                for kc in range(8):
                    s.op("pe", lambda e, kc=kc, f=f, bv=bv: e.matmul(bv.ap[:, 0:Tt], lhsT=wup.ap[:, kc, DFF + f * 128:DFF + (f + 1) * 128],
                                                                     rhs=h2T.ap[:, kc, 0:Tt], start=(kc == 0), stop=(kc == 7)),
                         reads=[wup, h2T], writes=[bv])
                ub = ubuf[rr2("ub", 2)]
                ac = acc[rr2("acc", 2)]
                gg = ge[rr2("ge", 2)]
                s.op("act", lambda e, ub=ub, f=f: e.activation(out=ub.ap[:, 0:2], in_=carry.ap[:, f, :], func=AF.Copy), reads=[carry], writes=[ub])
                s.op("act", lambda e, ub=ub, bu=bu: e.activation(out=ub.ap[:, 2:2 + Tt], in_=bu.ap[:, 0:Tt], func=AF.Copy), reads=[bu], writes=[ub])
                s.op("act", lambda e, bu=bu, f=f: e.activation(out=carry.ap[:, f, :], in_=bu.ap[:, Tt - 2:Tt], func=AF.Copy), reads=[bu], writes=[carry])
                s.op("act", lambda e, bu=bu, ac=ac, f=f: e.activation(out=ac.ap[:, 0:Tt], in_=bu.ap[:, 0:Tt], func=AF.Identity,
                                                                      bias=cbias.ap[:, f:f + 1], scale=cw.ap[:, f, 2:3]),
                     reads=[bu, cw, cbias], writes=[ac])
                s.op("dve", lambda e, ub=ub, ac=ac, f=f: e.scalar_tensor_tensor(out=ac.ap[:, 0:Tt], in0=ub.ap[:, 1:1 + Tt], scalar=cw.ap[:, f, 1:2],
                                                                                in1=ac.ap[:, 0:Tt], op0=ALU.mult, op1=ALU.add),
                     reads=[ub, cw, ac], writes=[ac])
                s.op("dve", lambda e, ub=ub, ac=ac, f=f: e.scalar_tensor_tensor(out=ac.ap[:, 0:Tt], in0=ub.ap[:, 0:Tt], scalar=cw.ap[:, f, 0:1],
                                                                                in1=ac.ap[:, 0:Tt], op0=ALU.mult, op1=ALU.add),
                     reads=[ub, cw, ac], writes=[ac])
                def tail(ac=ac, gg=gg, bv=bv, f=f):
                    s.op("act", lambda e: e.activation(out=gg.ap[:, 0:Tt], in_=ac.ap[:, 0:Tt], func=AF.Gelu), reads=[ac], writes=[gg])
                    s.op("dve", lambda e: e.tensor_tensor(out=gT.ap[:, f, 0:Tt], in0=gg.ap[:, 0:Tt], in1=bv.ap[:, 0:Tt], op=ALU.mult),
                         reads=[gg, bv], writes=[gT])
                if pend_tail[0] is not None:
                    pend_tail[0]()
                pend_tail[0] = tail
            pend_tail[0]()
            pend_tail[0] = None
            if tl.conv_dst is not None:
                for c0 in range(0, DFF, 512):
                    n = min(512, DFF - c0)
                    bk = pbank[4 + rr2("cv", 2)]
                    for kc in range(8):
                        s.op("pe", lambda e, kc=kc, c0=c0, n=n, bk=bk: e.matmul(bk.ap[0:2, 0:n], lhsT=h2T.ap[:, kc, Tt - 2:Tt], rhs=wup.ap[:, kc, c0:c0 + n],
                                                                                start=(kc == 0), stop=(kc == 7)), reads=[wup, h2T], writes=[bk])
                    s.op("act", lambda e, n=n, bk=bk: e.activation(out=cvst.ap[0:2, 0:n], in_=bk.ap[0:2, 0:n], func=AF.Copy), reads=[bk], writes=[cvst])
                    dst = tl.conv_dst[:, c0:c0 + n]
                    s.dma(mchan(), lambda e, n=n, dst=dst: e.dma_start(out=dst, in_=cvst.ap[0:2, 0:n]), reads=[cvst])

        def p2_down(tl, j, mid=None):
            nq = tl.nq
            zp = ppairs[rr2("zp", 2)]
            for hf in range(2):
                for kc in range(NFT):
                    s.op("pe", lambda e, kc=kc, hf=hf: e.matmul(zp.ap[0:nq, hf * 512:(hf + 1) * 512], lhsT=gT.ap[:, kc, j * nq:(j + 1) * nq],
                                                                rhs=wdn.ap[:, kc, hf * 512:(hf + 1) * 512],
                                                                start=(kc == 0), stop=(kc == NFT - 1)), reads=[gT, wdn], writes=[zp])
            if mid is not None:
                mid()
            ssT, rsT = new_ss()
            jk = tmpz2.ap.bitcast(BF16)
            for hf in range(2):
                s.op("act", lambda e, hf=hf: e.activation(out=jk[0:nq, hf * 512:(hf + 1) * 512], in_=zp.ap[0:nq, hf * 512:(hf + 1) * 512],
                                                          func=AF.Square, accum_out=ssT.ap[0:nq, hf:hf + 1]), reads=[zp], writes=[ssT, tmpz2])
            emit_rstd(ssT, rsT, nq)
            xr = tl.xr_of[j]
            for hf in range(2):
                s.op("dve", lambda e, hf=hf: e.scalar_tensor_tensor(out=tmpz2.ap[0:nq, hf * 512:(hf + 1) * 512], in0=zp.ap[0:nq, hf * 512:(hf + 1) * 512],
                                                                    scalar=rsT.ap[0:nq], in1=g2post.ap[0:nq, hf * 512:(hf + 1) * 512],
                                                                    op0=ALU.mult, op1=ALU.mult), reads=[zp, rsT, g2post], writes=[tmpz2])
            s.op("dve", lambda e: e.tensor_tensor(out=xr.ap[0:nq, :], in0=tmpz2.ap[0:nq, :], in1=xr.ap[0:nq, :], op=ALU.add),
                 reads=[tmpz2, xr], writes=[xr])
            s.dma(mchan(), lambda e: e.dma_start(out=tl.ydst(j), in_=xr.ap[0:nq, :]), reads=[xr])
            if j + 2 < tl.nsub:
                p2_load_res(tl, j + 2)

        tiles2 = []
        if sample:
            tiles2.append(Tile2(lambda j: x_x1s[:, :], lambda j: o_ys[:, :], 1, 16, o_convs, "state"))
        ntile2 = S // T2
        for sq in range(NSEQ):
            for t in range(ntile2):
                tiles2.append(Tile2(lambda j, sq=sq, t=t: x_x1p[sq, t * T2 + j * 128: t * T2 + (j + 1) * 128, :],
                                    lambda j, sq=sq, t=t: o_yp[sq, t * T2 + j * 128: t * T2 + (j + 1) * 128, :],
                                    T2 // 128, 128, (o_convp[sq] if t == ntile2 - 1 else None), ("zero" if t == 0 else None)))
        tl0 = tiles2[0]
        for j in range(min(2, tl0.nsub)):
            p2_load_x(tl0, j)
        for j in range(tl0.nsub):
            p2_stage0a(tl0, j)
            p2_stage0b(tl0, j)
        for i, tl in enumerate(tiles2):
            nxt = tiles2[i + 1] if i + 1 < len(tiles2) else None
            if nxt is not None:
                for j in range(min(2, nxt.nsub)):
                    p2_load_x(nxt, j)
            p2_ffn(tl)
            if nxt is not None:
                p2_stage0a(nxt, 0)
            for j in range(tl.nsub):
                def mid(j=j, nxt=nxt):
                    if nxt is not None and j + 1 < nxt.nsub:
                        p2_stage0a(nxt, j + 1)
                p2_down(tl, j, mid)
                if nxt is not None and j < nxt.nsub:
                    p2_stage0b(nxt, j)
            if nxt is not None:
                for j in range(tl.nsub, nxt.nsub):
                    p2_stage0b(nxt, j)
                    if j + 1 < nxt.nsub:
                        p2_stage0a(nxt, j + 1)

        s.final_wait()
        s.emit(nc, stack)
    return nc


def _consts(S):
    bf = ml_dtypes.bfloat16
    half = 32
    inv = 1.0 / (10000.0 ** (np.arange(half, dtype=np.float32) * np.float32(2.0 / 64)))
    inv = inv.astype(np.float32)

    def fm(pos):
        ang = pos.astype(np.float32)[None, :] * inv[np.arange(128) % 32][:, None]
        return np.stack([np.cos(ang), np.sin(ang)], axis=1).astype(np.float32)

    def tm(pos):
        ang = pos.astype(np.float32)[:, None] * inv[None, :]
        return np.concatenate([np.cos(ang), np.sin(ang)], axis=1).astype(np.float32)

    c = {}
    c["ropefm"] = np.ascontiguousarray(fm(np.arange(S)))
    c["ropefm_s"] = np.ascontiguousarray(fm(1024 + np.arange(16)))
    c["ropetm"] = np.ascontiguousarray(tm(np.arange(S - 128, S)))
    c["ropetm_s"] = np.ascontiguousarray(tm(1024 + np.arange(16)))
    c["ident"] = np.eye(128, dtype=np.float32).astype(bf)
    c["jmat"] = np.ascontiguousarray(np.eye(128, dtype=np.float32)[::-1]).astype(bf)
    perms = np.zeros((128, 5, 128), dtype=np.float32)
    for m in range(128):
        d = m % 64
        base = m - d
        if d < 32:
            perms[base + d + 32, 0, m] = -1.0
        else:
            perms[base + d - 32, 0, m] = 1.0
        for g in range(2):
            perms[g * 64 + d, 1 + g, m] = 1.0
            if d < 32:
                perms[g * 64 + d + 32, 3 + g, m] = -1.0
            else:
                perms[g * 64 + d - 32, 3 + g, m] = 1.0
    c["perms"] = perms.reshape(128, 5 * 128).astype(bf)
    ea = np.ones((128, 2, 2, 128), dtype=np.float32)
    ea[0:64, :, 0, 64:128] = 0.0
    ea[64:128, :, 1, 0:64] = 0.0
    c["ea"] = ea.reshape(128, 512).astype(bf)
    return c


_NC_CACHE = {}


def _run(inputs, S, NSEQ_TOTAL, n_cores, sample=True):
    nseq = NSEQ_TOTAL // n_cores
    key = (S, nseq, sample)
    if key not in _NC_CACHE:
        _NC_CACHE[key] = build_nc(S, nseq, sample)
    nc = _NC_CACHE[key]
    f = lambda a: np.ascontiguousarray(np.asarray(a, dtype=np.float32))
    consts = _consts(S)
    shared = dict(
        w_in=f(inputs["w_in"][0]), w_oa=f(inputs["w_oa"][0]), w_ob=f(inputs["w_ob"][0]), w_out=f(inputs["w_out"][0]),
        w_up=f(inputs["w_up"][0]), w_down=f(inputs["w_down"][0]),
        gpre=f(np.asarray(inputs["g_mix_pre"][0]).reshape(8, 128).T), gpost=f(np.asarray(inputs["g_mix_post"][0]).reshape(1, D)),
        g2pre=f(np.asarray(inputs["g_ffn_pre"][0]).reshape(8, 128).T), g2post=f(np.asarray(inputs["g_ffn_post"][0]).reshape(1, D)),
        sinks=f(np.asarray(inputs["sinks"][0]).reshape(1, 8)),
        trev=f(np.asarray(inputs["rel_bias"][0])[:, ::-1]),
        cw=f(np.asarray(inputs["conv_w"][0]).reshape(3, NFT, 128).transpose(2, 1, 0).reshape(128, NFT * 3)),
        cb=f(np.asarray(inputs["conv_b"][0]).reshape(NFT, 128).T),
        **consts,
    )
    xp = np.asarray(inputs["x_prompt"], dtype=np.float32)
    in_maps = []
    for c in range(n_cores):
        m = dict(shared)
        m["xp"] = np.ascontiguousarray(xp[c * nseq:(c + 1) * nseq])
        m["xs"] = f(inputs["x_sample"][c])
        m["cak"] = f(np.asarray(inputs["cache_a_k"][0, c]).reshape(128, 128))
        m["cav"] = f(np.asarray(inputs["cache_a_v"][0, c]).reshape(128, 128))
        m["cbk"] = f(np.asarray(inputs["cache_b_k"][0, c]).reshape(512, 512))
        m["cbv"] = f(np.asarray(inputs["cache_b_v"][0, c]).reshape(512, 512))
        m["sconv"] = f(np.asarray(inputs["state_conv"][0, c]).reshape(2, NFT, 128).transpose(2, 1, 0).reshape(128, NFT * 2))
        in_maps.append(m)
    res = run_bass_kernel_spmd(nc, in_maps, core_ids=list(range(n_cores)))
    R = res.results
    cat = lambda k: np.concatenate([np.asarray(r[k], dtype=np.float32) for r in R], axis=0)
    stk = lambda k: np.stack([np.asarray(r[k], dtype=np.float32) for r in R], axis=0)
    KA_ROWS = min(128, S)
    KB_ROWS = min(512, S)
    B = NSEQ_TOTAL
    return (
        cat("yp"),
        stk("ys"),
        cat("akp").reshape(1, B, KA_ROWS, 2, 64),
        cat("avp").reshape(1, B, KA_ROWS, 2, 64),
        cat("bkp").reshape(1, B, KB_ROWS, 8, 64),
        cat("bvp").reshape(1, B, KB_ROWS, 8, 64),
        cat("convp").reshape(1, B, 2, DFF),
        stk("aks").reshape(1, n_cores, 16, 2, 64),
        stk("avs").reshape(1, n_cores, 16, 2, 64),
        stk("bks").reshape(1, n_cores, 16, 8, 64),
        stk("bvs").reshape(1, n_cores, 16, 8, 64),
        stk("convs").reshape(1, n_cores, 2, DFF),
    )


def kernel(**inputs):
    xp = inputs["x_prompt"]
    return _run(inputs, S=int(xp.shape[1]), NSEQ_TOTAL=int(xp.shape[0]), n_cores=8, sample=True)
```
